# Optimizing a Trainium2 kernel written in Bass

```python
import math
import jax, jax.numpy as jnp
from jax import lax
import numpy as np

D_MODEL = 4096
BATCH = 4
SEQ = 2048
DEPTH = 4
DEC_BATCH = 32
DEC_SEQ = 4
PAST_LEN = 8192
PAGE_SIZE = 128

N_EVEN = (DEPTH + 1) // 2
N_ODD = DEPTH // 2
EPS = 1e-6
NEG_INF = -1e30
MIX_WIDTH = D_MODEL
A_WIDTH = MIX_WIDTH // 2
A_HEADS = 8
A_HEAD_DIM = A_WIDTH // A_HEADS
CHUNK = 128
B_WIDTH = MIX_WIDTH - A_WIDTH
CONV_WIDTH = 3
EVEN_PROJ = 2 * A_WIDTH + 3 * B_WIDTH
HEAD_DIM = 64
N_HEADS = D_MODEL // HEAD_DIM
N_KV_HEADS = 8
GQA_GROUP = N_HEADS // N_KV_HEADS
WINDOW = 128
ATTN_BLOCK = 128
QKV_PROJ = (N_HEADS + 2 * N_KV_HEADS) * HEAD_DIM
N_BUCKETS = 32
MAX_DISTANCE = 128
D_FF = -(-8 * D_MODEL // (3 * 256)) * 256

kernel_name = "hybrid_sgu_conv_swa_decoder_step"


def rms_norm(x, g):
    xf = x.astype(jnp.float32)
    y = xf * lax.rsqrt(jnp.mean(xf * xf, axis=-1, keepdims=True) + EPS)
    return (y * g.astype(jnp.float32)).astype(x.dtype)


def swiglu(h, wg, wu, wd):
    return (jax.nn.silu(h @ wg) * (h @ wu)) @ wd


def even_inputs(h, w_in):
    proj = h @ w_in
    u, v, xb, gc, gb = jnp.split(
        proj, [A_WIDTH, 2 * A_WIDTH, 2 * A_WIDTH + B_WIDTH, 2 * A_WIDTH + 2 * B_WIDTH], axis=-1)
    u = jax.nn.gelu(u)
    v = jax.nn.gelu(v).reshape(*h.shape[:-1], A_HEADS, A_HEAD_DIM)
    z = gc * xb
    return u, v, z, gb


def spatial_gate(u, v, w_s, b_s):
    L = v.shape[-3]
    w = jnp.tril(w_s[:, :L, :L])
    mixed = jnp.einsum('hij,ncjhd->ncihd', w, v) + b_s[:, :L].T[:, :, None]
    return u * mixed.reshape(u.shape)


def causal_dwconv(z_ext, w):
    return lax.conv_general_dilated(
        z_ext, w[:, None, :].astype(z_ext.dtype), window_strides=(1,), padding='VALID',
        dimension_numbers=('NWC', 'WIO', 'NWC'), feature_group_count=z_ext.shape[-1])


def even_mixer_prompt(h, w_in, w_out, w_s, b_s, conv_w):
    n, s, _ = h.shape
    nc = s // CHUNK
    u, v, z, gb = even_inputs(h, w_in)
    a = spatial_gate(u.reshape(n, nc, CHUNK, A_WIDTH),
                     v.reshape(n, nc, CHUNK, A_HEADS, A_HEAD_DIM), w_s, b_s).reshape(n, s, A_WIDTH)
    z_ext = jnp.pad(z, ((0, 0), (CONV_WIDTH - 1, 0), (0, 0)))
    b = gb * causal_dwconv(z_ext, conv_w)
    y = jnp.concatenate([a, b], axis=-1) @ w_out
    return y, z[:, s - (CONV_WIDTH - 1):]


def even_mixer_sample(h, conv_state, w_in, w_out, w_s, b_s, conv_w):
    u, v, z, gb = even_inputs(h, w_in)
    a = spatial_gate(u[:, None], v[:, None], w_s, b_s)[:, 0]
    z_ext = jnp.concatenate([conv_state.astype(z.dtype), z], axis=1)
    b = gb * causal_dwconv(z_ext, conv_w)
    y = jnp.concatenate([a, b], axis=-1) @ w_out
    return y, z_ext[:, z_ext.shape[1] - (CONV_WIDTH - 1):], v


def t5_bucket(dist):
    max_exact = N_BUCKETS // 2
    d = jnp.maximum(dist, max_exact).astype(jnp.float32)
    large = max_exact + (jnp.log(d / max_exact) / math.log(MAX_DISTANCE / max_exact)
                         * (N_BUCKETS - max_exact)).astype(jnp.int32)
    return jnp.where(dist < max_exact, dist, jnp.minimum(large, N_BUCKETS - 1))


def dist_bias(dist, rel_bias):
    bucket = t5_bucket(jnp.clip(dist, 0, WINDOW - 1))
    b = rel_bias.astype(jnp.float32)[bucket]
    b = b.reshape(*dist.shape, N_KV_HEADS, GQA_GROUP)
    return jnp.transpose(b, (2, 3, 0, 1))


def qkv_split(h, w_qkv):
    proj = h @ w_qkv
    q, k, v = jnp.split(proj, [N_HEADS * HEAD_DIM, (N_HEADS + N_KV_HEADS) * HEAD_DIM], axis=-1)
    lead = h.shape[:-1]
    return (q.reshape(*lead, N_KV_HEADS, GQA_GROUP, HEAD_DIM),
            k.reshape(*lead, N_KV_HEADS, HEAD_DIM),
            v.reshape(*lead, N_KV_HEADS, HEAD_DIM))


def sink_attend(q, k, v, bias, valid, sink):
    s = jnp.einsum('...qkgd,...skd->...kgqs', q, k,
                   preferred_element_type=jnp.float32) * (HEAD_DIM ** -0.5) + bias
    s = jnp.where(valid, s, NEG_INF)
    sk = sink.astype(jnp.float32).reshape(N_KV_HEADS, GQA_GROUP, 1, 1)
    m = jnp.maximum(jnp.max(s, axis=-1, keepdims=True), sk)
    p = jnp.exp(s - m)
    w = p / (jnp.sum(p, axis=-1, keepdims=True) + jnp.exp(sk - m))
    return jnp.einsum('...kgqs,...skd->...qkgd', w.astype(v.dtype), v)


def odd_mixer_prompt(h, w_qkv, w_o, sink, rel_bias):
    n, s, _ = h.shape
    nb = s // ATTN_BLOCK
    q, k, v = qkv_split(h, w_qkv)
    qb = q.reshape(n, nb, ATTN_BLOCK, N_KV_HEADS, GQA_GROUP, HEAD_DIM)

    def band(t):
        tb = t.reshape(n, nb, ATTN_BLOCK, N_KV_HEADS, HEAD_DIM)
        prev = jnp.concatenate([jnp.zeros_like(tb[:, :1]), tb[:, :-1]], axis=1)
        return jnp.concatenate([prev, tb], axis=2)

    kb, vb = band(k), band(v)
    dist = jnp.arange(ATTN_BLOCK)[:, None] + ATTN_BLOCK - jnp.arange(2 * ATTN_BLOCK)[None, :]
    key_pos = (jnp.arange(nb)[:, None] - 1) * ATTN_BLOCK + jnp.arange(2 * ATTN_BLOCK)[None, :]
    valid = (dist >= 0) & (dist < WINDOW) & (key_pos[:, None, :] >= 0)
    o = sink_attend(qb, kb, vb, dist_bias(dist, rel_bias), valid[:, None, None], sink)
    y = o.reshape(n, s, N_HEADS * HEAD_DIM) @ w_o
    return y, k[:, s - WINDOW:], v[:, s - WINDOW:]


def odd_mixer_sample(h, cache_k, cache_v, w_qkv, w_o, sink, rel_bias):
    n, t, _ = h.shape
    r = cache_k.shape[1]
    q, k, v = qkv_split(h, w_qkv)
    k_all = jnp.concatenate([cache_k.astype(k.dtype), k], axis=1)
    v_all = jnp.concatenate([cache_v.astype(v.dtype), v], axis=1)
    dist = jnp.arange(t)[:, None] + r - jnp.arange(r + t)[None, :]
    valid = (dist >= 0) & (dist < WINDOW)
    o = sink_attend(q, k_all, v_all, dist_bias(dist, rel_bias), valid, sink)
    y = o.reshape(n, t, N_HEADS * HEAD_DIM) @ w_o
    return y, k_all[:, k_all.shape[1] - WINDOW:], v_all[:, v_all.shape[1] - WINDOW:]


def setup_inputs(seed: int = 0) -> dict:
    key = jax.random.key(seed)
    ks = jax.random.split(key, 22)
    f32 = jnp.float32

    def nrm(k, shape, scale):
        return jax.random.normal(k, shape, f32) * scale

    win_rows = min(WINDOW, PAST_LEN)
    return {
        "x_prompt": nrm(ks[0], (BATCH, SEQ, D_MODEL), 1.0),
        "x_sample": nrm(ks[1], (DEC_BATCH, DEC_SEQ, D_MODEL), 1.0),
        "state_conv": nrm(ks[2], (N_EVEN, DEC_BATCH, CONV_WIDTH - 1, B_WIDTH), 0.5),
        "cache_win_k": nrm(ks[3], (N_ODD, DEC_BATCH, win_rows, N_KV_HEADS, HEAD_DIM), 1.0),
        "cache_win_v": nrm(ks[4], (N_ODD, DEC_BATCH, win_rows, N_KV_HEADS, HEAD_DIM), 1.0),
        "norm_mix_pre": 1.0 + nrm(ks[5], (DEPTH, D_MODEL), 0.05),
        "norm_mix_post": 1.0 + nrm(ks[6], (DEPTH, D_MODEL), 0.05),
        "norm_ffn_pre": 1.0 + nrm(ks[7], (DEPTH, D_MODEL), 0.05),
        "norm_ffn_post": 1.0 + nrm(ks[8], (DEPTH, D_MODEL), 0.05),
        "w_in_even": nrm(ks[9], (N_EVEN, D_MODEL, EVEN_PROJ), D_MODEL ** -0.5),
        "w_out_even": nrm(ks[10], (N_EVEN, MIX_WIDTH, D_MODEL), MIX_WIDTH ** -0.5),
        "sgu_w": nrm(ks[11], (N_EVEN, A_HEADS, CHUNK, CHUNK), CHUNK ** -0.5),
        "sgu_b": 1.0 + nrm(ks[12], (N_EVEN, A_HEADS, CHUNK), 0.1),
        "conv_w": nrm(ks[13], (N_EVEN, CONV_WIDTH, B_WIDTH), CONV_WIDTH ** -0.5),
        "w_qkv_odd": nrm(ks[14], (N_ODD, D_MODEL, QKV_PROJ), D_MODEL ** -0.5),
        "w_o_odd": nrm(ks[15], (N_ODD, N_HEADS * HEAD_DIM, D_MODEL), (N_HEADS * HEAD_DIM) ** -0.5),
        "attn_sinks": nrm(ks[16], (N_ODD, N_HEADS), 1.0),
        "rel_bias": nrm(ks[17], (N_BUCKETS, N_HEADS), 0.5),
        "ffn_w_gate": nrm(ks[18], (DEPTH, D_MODEL, D_FF), D_MODEL ** -0.5),
        "ffn_w_up": nrm(ks[19], (DEPTH, D_MODEL, D_FF), D_MODEL ** -0.5),
        "ffn_w_down": nrm(ks[20], (DEPTH, D_FF, D_MODEL), D_FF ** -0.5),
    }


def reference(x_prompt, x_sample, state_conv, cache_win_k, cache_win_v,
              norm_mix_pre, norm_mix_post, norm_ffn_pre, norm_ffn_post,
              w_in_even, w_out_even, sgu_w, sgu_b, conv_w,
              w_qkv_odd, w_o_odd, attn_sinks, rel_bias,
              ffn_w_gate, ffn_w_up, ffn_w_down):
    xp, xs = x_prompt, x_sample
    conv_p, conv_s, chunk_v_s = [], [], []
    kp, vp, ks, vs = [], [], [], []
    for layer in range(DEPTH):
        i = layer // 2
        hp = rms_norm(xp, norm_mix_pre[layer])
        hs = rms_norm(xs, norm_mix_pre[layer])
        if layer % 2 == 0:
            yp, cp = even_mixer_prompt(hp, w_in_even[i], w_out_even[i], sgu_w[i], sgu_b[i], conv_w[i])
            ys, cs, v_new = even_mixer_sample(hs, state_conv[i], w_in_even[i], w_out_even[i],
                                              sgu_w[i], sgu_b[i], conv_w[i])
            conv_p.append(cp)
            conv_s.append(cs)
            chunk_v_s.append(v_new)
        else:
            yp, k_p, v_p = odd_mixer_prompt(hp, w_qkv_odd[i], w_o_odd[i], attn_sinks[i], rel_bias)
            ys, k_s, v_s = odd_mixer_sample(hs, cache_win_k[i], cache_win_v[i], w_qkv_odd[i],
                                            w_o_odd[i], attn_sinks[i], rel_bias)
            kp.append(k_p)
            vp.append(v_p)
            ks.append(k_s)
            vs.append(v_s)
        xp = xp + rms_norm(yp, norm_mix_post[layer])
        xs = xs + rms_norm(ys, norm_mix_post[layer])
        xp = xp + rms_norm(swiglu(rms_norm(xp, norm_ffn_pre[layer]), ffn_w_gate[layer],
                                  ffn_w_up[layer], ffn_w_down[layer]), norm_ffn_post[layer])
        xs = xs + rms_norm(swiglu(rms_norm(xs, norm_ffn_pre[layer]), ffn_w_gate[layer],
                                  ffn_w_up[layer], ffn_w_down[layer]), norm_ffn_post[layer])
    return (xp, xs, jnp.stack(conv_p), jnp.stack(conv_s), jnp.stack(kp), jnp.stack(vp),
            jnp.stack(ks), jnp.stack(vs), jnp.stack(chunk_v_s))
```

```python
import math
import numpy as np
import ml_dtypes
import concourse.bass as bass
import concourse.mybir as mybir
from concourse.bass_utils import run_bass_kernel_spmd

F32 = mybir.dt.float32
BF16 = mybir.dt.bfloat16
AF = mybir.ActivationFunctionType
ALU = mybir.AluOpType
AX = mybir.AxisListType

D = 4096
KC = 32
EPS = 1e-6
NEG = -1e30
SEM_LIMIT = 12000
NSLOT = 3
USE_WCACHE = True


class Sem:
    def __init__(self, h, idx, eng):
        self.h, self.idx, self.eng = h, idx, eng


class Plan:
    ENGS = ("sp", "pool", "dve", "act", "pe")

    def __init__(self, nc):
        self.nc = nc
        self.lists = {e: [] for e in self.ENGS}
        self.nsem = 0
        self.cnt = {}
        self.buf = {}
        self.waited = {e: {} for e in self.ENGS}
        self.collect = False
        self.dma_pool = []
        self.dma_rr = 0
        self.out_events = []

    def new_sem(self, eng):
        h = self.nc.alloc_semaphore(f"s{self.nsem}")
        s = Sem(h, self.nsem, eng)
        self.nsem += 1
        return s

    def next_ev(self, eng):
        c = self.cnt.get(eng)
        if c is None or c[1] >= SEM_LIMIT:
            c = [self.new_sem(eng), 0]
            self.cnt[eng] = c
        c[1] += 1
        return (c[0], c[1])

    def _deps(self, reads, writes):
        d = []
        for k in reads:
            b = self.buf.get(k)
            if b and b[0] is not None:
                d.append(b[0])
        for k in writes:
            b = self.buf.get(k)
            if b:
                if b[0] is not None:
                    d.append(b[0])
                d.extend(b[1].values())
        return d

    def _commit(self, ev, reads, writes):
        for k in reads:
            b = self.buf.setdefault(k, [None, {}])
            o = b[1].get(ev[0].idx)
            if o is None or o[1] < ev[1]:
                b[1][ev[0].idx] = ev
        for k in writes:
            self.buf[k] = [ev, {}]

    def _waits(self, eng, deps):
        for (s, v) in deps:
            if s.eng == "pe" and eng == "pe":
                continue
            if self.waited[eng].get(s.idx, 0) >= v:
                continue
            self.waited[eng][s.idx] = v
            self.lists[eng].append(("w", s, v))

    def op(self, eng, fn, reads=(), writes=(), inc=True):
        if self.collect:
            return None
        if "S" not in writes and "S" not in reads:
            reads = tuple(reads) + ("S",)
        if eng in ("act", "dve"):
            pk = tuple(k for k in reads if len(k) == 2 and k[0] == "P" and k[1].isdigit())
            if pk:
                reads = tuple(k for k in reads if k not in pk)
                writes = tuple(writes) + pk
        self._waits(eng, self._deps(reads, writes))
        ev = self.next_ev(eng) if inc else None
        self.lists[eng].append(("o", fn, ev, 1))
        if ev is not None:
            self._commit(ev, reads, writes)
        return ev

    def dma(self, q, out, in_, reads=(), writes=(), sem=None, is_out=False, noauto=False):
        if self.collect:
            return None
        if sem is None and "S" not in reads and not noauto:
            reads = tuple(reads) + ("S",)
        if sem is None:
            k = self.dma_rr % 24
            if k >= len(self.dma_pool):
                self.dma_pool.append([self.new_sem(None), 0])
            ent = self.dma_pool[k]
            self.dma_rr += 1
        else:
            ent = sem
        deps = self._deps(reads, writes)
        if ent[1] > 0:
            deps.append((ent[0], ent[1]))
        self._waits(q, deps)
        ent[1] += 16
        ev = (ent[0], ent[1])
        self.lists[q].append(("o", lambda e, o=out, i=in_: e.dma_start(out=o, in_=i), ev, 16))
        self._commit(ev, reads, writes)
        if is_out:
            self.out_events.append(ev)
        return ev

    def barrier(self, key):
        self.op("dve", lambda e: e.memset(self.scr[0:1, 0:1], 0.0), reads=(), writes=(key,))

    def emit(self, block):
        names = {"sp": "sync", "pool": "gpsimd", "dve": "vector", "act": "scalar", "pe": "tensor"}
        last = {}
        for (s, v) in self.out_events:
            if last.get(s.idx, (None, 0))[1] < v:
                last[s.idx] = (s, v)
        for (s, v) in last.values():
            self.lists["sp"].append(("w", s, v))
        for en in self.ENGS:
            items = self.lists[en]

            def body(eng, items=items):
                for it in items:
                    if it[0] == "w":
                        eng.wait_ge(it[1].h, it[2])
                    else:
                        ins = it[1](eng)
                        if it[2] is not None:
                            ins.then_inc(it[2][0].h, it[3])
            getattr(block, names[en])(body)


def t5_bucket_np(dist):
    max_exact = 16
    d = np.maximum(dist, max_exact).astype(np.float32)
    large = max_exact + (np.log(d / max_exact) / math.log(128 / max_exact) * (32 - max_exact)).astype(np.int32)
    return np.where(dist < max_exact, dist, np.minimum(large, 31))


def make_consts():
    c = np.zeros((128, 1280), np.float32)
    c[:, 0:128] = np.eye(128, dtype=np.float32)
    j = np.arange(128)[:, None]
    i = np.arange(128)[None, :]
    c[:, 128:256] = (j <= i).astype(np.float32)
    cc = np.arange(383)[None, :]
    c[:, 256:639] = (cc == (np.arange(128)[:, None] + 127)).astype(np.float32)
    q = np.arange(128)[:, None]
    k = np.arange(256)[None, :]
    dist = q + 128 - k
    valid = (dist >= 0) & (dist < 128)
    c[:, 639:895] = np.where(valid, 0.0, NEG).astype(np.float32)
    fm = np.zeros((128, 256), np.float32)
    fm[:, 0:128] = NEG
    c[:, 895:1151] = fm
    bk = t5_bucket_np(np.arange(128))
    oh = np.zeros((32, 128), np.float32)
    oh[bk, np.arange(128)] = 1.0
    c[0:32, 1151:1279] = oh
    return c


class Cfg:
    def __init__(self, nstep=5, depth=4, dff=11008, nblk=2, nsamp=4):
        self.nstep, self.depth, self.dff, self.nblk, self.nsamp = nstep, depth, dff, nblk, nsamp
        self.n_even = (depth + 1) // 2
        self.n_odd = depth // 2
        self.fc = dff // 128
        self.tp = nblk * 128
        self.t = self.tp + 4


def build(cfg):
    import os as _os
    nc = bass.Bass("TRN2", target_bir_lowering=False)
    P = Plan(nc)
    NST, DEPTH, DFF, NBLK, NSAMP = cfg.nstep, cfg.depth, cfg.dff, cfg.nblk, cfg.nsamp
    NE, NO, FC, TP, TT = cfg.n_even, cfg.n_odd, cfg.fc, cfg.tp, cfg.t
    NOD = max(NO, 1)

    def din(name, shape):
        return nc.dram_tensor(name, list(shape), F32, kind="ExternalInput").ap()

    def dout(name, shape):
        return nc.dram_tensor(name, list(shape), F32, kind="ExternalOutput").ap()

    xp = din("xp", [NST * TP, D])
    xs = din("xs", [NSAMP * 4, D])
    scv = din("scv", [NE * NSAMP * 2, 2048])
    cwk = din("cwk", [NOD * NSAMP * 128, 512])
    cwv = din("cwv", [NOD * NSAMP * 128, 512])
    gall = din("gall", [16, D])
    w_in = din("w_in", [NE * D, 10240])
    w_out = din("w_out", [NE * D, D])
    sguw = din("sguw", [NE * 8 * 128, 128])
    sgub = din("sgub", [1, NE * 8 * 128])
    convw = din("convw", [NE * 3, 2048])
    w_qkv = din("w_qkv", [NOD * D, 5120])
    w_o = din("w_o", [NOD * D, D])
    sinks = din("sinks", [1, NOD * 64])
    relb = din("relb", [32, 64])
    wg = din("wg", [DEPTH * D, DFF])
    wu = din("wu", [DEPTH * D, DFF])
    wd = din("wd", [DEPTH * DFF, D])
    cst = din("cst", [128, 1280])

    o_yp = dout("o_yp", [NST * TP, D])
    o_ys = dout("o_ys", [NSAMP * 4, D])
    o_cp = dout("o_cp", [NE * 2, 2048])
    o_cs = dout("o_cs", [NE * NSAMP * 2, 2048])
    o_kp = dout("o_kp", [NOD * 128, 512])
    o_vp = dout("o_vp", [NOD * 128, 512])
    o_ks = dout("o_ks", [NOD * NSAMP * 128, 512])
    o_vs = dout("o_vs", [NOD * NSAMP * 128, 512])
    o_cv = dout("o_cv", [NE * NSAMP * 4, 2048])
    bm_dram = nc.dram_tensor("bm_dram", [8 * 128, 2048], F32, kind="Internal").ap()

    def sb(name, shape, dt):
        return nc.alloc_sbuf_tensor(name, list(shape), dt)

    xT = sb("xT", [128, KC, TT], F32)
    yT = sb("yT", [128, KC, TT], F32)
    hT = sb("hT", [128, KC, TT], BF16)
    mixT = sb("mixT", [128, KC, TT], BF16)
    slots = sb("slots", [128, NSLOT, 8, 512], BF16)
    SBYTES = 57600
    SR = sb("SR", [128, SBYTES // 4], F32)
    ident = sb("ident", [128, 128], F32)
    identb = sb("identb", [128, 128], BF16)
    ones32 = sb("ones32", [128, 128], F32)
    onesb = sb("onesb", [1, 128], BF16)
    fmask = sb("fmask", [128, 256], F32)
    gT = sb("gT", [128, KC, 16], F32)
    cwT = sb("cwT", [128, 16, 8], F32)
    scT = sb("scT", [128, 16, 16], F32)
    WtS = sb("WtS", [128, NE * 8, 128], BF16)
    bhi = sb("bhi", [1, NE * 8 * 128], BF16)
    blo = sb("blo", [1, NE * 8 * 128], BF16)
    sinkbc = sb("sinkbc", [128, NOD * 64], F32)
    zhalo = sb("zhalo", [128, NE, 16, 2], F32)
    kTh = sb("kTh", [128, NOD, 8, 128], BF16)
    vdh = sb("vdh", [128, NOD, 8, 128], BF16)
    rstd = sb("rstd", [128, TT], F32)
    small = sb("small", [128, 64], F32)
    P.scr = sb("scr", [1, 8], F32)
    ps = nc.alloc_psum_tensor("ps", [128, 8, 512], F32)
    psb3 = ps.bitcast(BF16)

    class Arena:
        def __init__(self, base, nwords):
            self.base, self.n, self.off = base, nwords, 0

        def reset(self):
            self.off = 0

        def take(self, shape, dt):
            per = int(np.prod(shape[1:]))
            words = (per * (2 if dt == BF16 else 4) + 3) // 4
            words = (words + 7) // 8 * 8
            assert self.off + words <= self.n, ("arena overflow", shape, self.off, words, self.n)
            v = self.base[:, self.off:self.off + words]
            self.off += words
            if dt == BF16:
                v = v.bitcast(BF16)[:, 0:per]
            else:
                v = v[:, 0:per]
            if len(shape) == 2:
                return v[0:shape[0], :]
            names = " ".join(f"a{i}" for i in range(len(shape) - 1))
            kw = {f"a{i}": shape[i + 1] for i in range(len(shape) - 1)}
            return v.rearrange(f"p ({names}) -> p {names}", **kw)[0:shape[0]]

    arS = Arena(SR, SBYTES // 4)
    arY = Arena(yT.rearrange("p a b -> p (a b)") if hasattr(yT, "rearrange") else yT[:].rearrange("p a b -> p (a b)"), KC * TT)

    bank_rr = [0]

    def banks(n):
        b0 = bank_rr[0]
        if b0 + n > 8:
            b0 = 0
        bank_rr[0] = (b0 + n) % 8
        return list(range(b0, b0 + n))

    bbank_rr = [0]

    def bbanks(n):
        b0 = bbank_rr[0]
        if b0 + n > 2:
            b0 = 0
        bbank_rr[0] = (b0 + n) % 2
        return list(range(b0, b0 + n))

    def bbkeys(bl):
        return tuple(f"B{b}" for b in bl)

    def bkeys(bl):
        return tuple(f"P{b}" for b in bl)

    jobs = []
    state = {"issued": 0, "next": 0}
    slot_sems = [[P.new_sem(None), 0] for _ in range(NSLOT)]

    wbf = [None, 0]

    def issue_upto(j):
        while state["issued"] <= min(j, len(jobs) - 1):
            jj = state["issued"]
            s = jj % NSLOT
            w, r0, nk, c0, ncol = jobs[jj]
            JPS = wbf[1]
            jl = jj % JPS
            cache = wbf[0][jl // 200][(jl % 200) * 128:(jl % 200 + 1) * 128, 0:nk * ncol].rearrange("p (k n) -> p k n", n=ncol)
            if jj < JPS or not USE_WCACHE:
                src = w[r0:r0 + nk * 128, c0:c0 + ncol].rearrange("(k p) n -> p k n", p=128)
                P.dma("pool", slots[:, s, 0:nk, 0:ncol], src, reads=(), writes=(f"slot{s}",), sem=slot_sems[s])
                if USE_WCACHE and NST > 1:
                    P.dma("sp", cache, slots[:, s, 0:nk, 0:ncol], reads=(f"slot{s}",), writes=(f"wb{jl}",), noauto=True)
            else:
                P.dma("pool", slots[:, s, 0:nk, 0:ncol], cache, reads=(f"wb{jl}",), writes=(f"slot{s}",), sem=slot_sems[s])
            state["issued"] += 1

    def next_job(w, r0, nk, c0, ncol):
        if P.collect:
            jobs.append((w, r0, nk, c0, ncol))
            return None
        j = state["next"]
        state["next"] += 1
        assert jobs[j][1:] == (r0, nk, c0, ncol), "job order mismatch"
        issue_upto(j + NSLOT - 1)
        return j % NSLOT

    def linear_fm(w, wr0, col0, ncols, inT, kcn, T, in_keys, evac):
        noc = ncols // 128
        bl = banks(noc)
        nj = (kcn + 7) // 8
        for jq in range(nj):
            nk = min(8, kcn - jq * 8)
            s = next_job(w, wr0 + jq * 1024, nk, col0, ncols)
            if P.collect:
                continue

            def fn(e, s=s, jq=jq, nk=nk):
                ins = None
                for oc in range(noc):
                    for k in range(nk):
                        kk = jq * 8 + k
                        ins = e.matmul(ps[:, bl[oc], 0:T], lhsT=slots[:, s, k, oc * 128:(oc + 1) * 128],
                                       rhs=inT[:, kk, 0:T], start=(kk == 0), stop=(kk == kcn - 1))
                return ins
            P.op("pe", fn, reads=(f"slot{s}",) + tuple(in_keys), writes=bkeys(bl))
        if not P.collect:
            for oc in range(noc):
                evac(oc, ps[:, bl[oc], 0:T], f"P{bl[oc]}")

    def linear_tm(w, wr0, col0, ncols, inT, kcn, tbs, in_keys, evac):
        bl = banks(len(tbs))
        nj = (kcn + 7) // 8
        for jq in range(nj):
            nk = min(8, kcn - jq * 8)
            s = next_job(w, wr0 + jq * 1024, nk, col0, ncols)
            if P.collect:
                continue

            def fn(e, s=s, jq=jq, nk=nk):
                ins = None
                for bi, (c0, n) in enumerate(tbs):
                    for k in range(nk):
                        kk = jq * 8 + k
                        ins = e.matmul(ps[0:n, bl[bi], 0:ncols], lhsT=inT[:, kk, c0:c0 + n],
                                       rhs=slots[:, s, k, 0:ncols], start=(kk == 0), stop=(kk == kcn - 1))
                return ins
            P.op("pe", fn, reads=(f"slot{s}",) + tuple(in_keys), writes=bkeys(bl))
        if not P.collect:
            for bi, (c0, n) in enumerate(tbs):
                evac(bi, ps[0:n, bl[bi], 0:ncols], f"P{bl[bi]}", n)

    def V(fn, r=(), w=()):
        return P.op("dve", fn, r, w)

    def A(fn, r=(), w=()):
        return P.op("act", fn, r, w)

    def PE(fn, r=(), w=()):
        return P.op("pe", fn, r, w)

    def gelu_evac(pin, pkey, out, outkey, tmp, tkey, extra_out=None):
        t0, t1 = tmp
        A(lambda e: e.activation(out=t0, in_=pin, func=AF.Square), (pkey, "S"), (tkey + "0",))
        V(lambda e: e.tensor_scalar(out=t0, in0=t0, scalar1=0.044715, scalar2=1.0, op0=ALU.mult, op1=ALU.add),
          (), (tkey + "0",))
        V(lambda e: e.tensor_tensor(out=t0, in0=t0, in1=pin, op=ALU.mult), (pkey,), (tkey + "0",))
        A(lambda e: e.activation(out=t1, in_=t0, func=AF.Sigmoid, scale=1.5957691216057308), (tkey + "0",), (tkey + "1",))
        if extra_out is not None:
            V(lambda e: e.tensor_tensor(out=extra_out[0], in0=t1, in1=pin, op=ALU.mult), (tkey + "1", pkey), (extra_out[1],))
        V(lambda e: e.tensor_tensor(out=out, in0=t1, in1=pin, op=ALU.mult), (tkey + "1", pkey), (outkey,))

    def rms_stats(src, srckey, sq, sqkey, T):
        A(lambda e: e.activation(out=sq[:, :, 0:T], in_=src[:, :, 0:T], func=AF.Square), (srckey,), (sqkey,))
        bl = banks(1)

        def fn(e):
            ins = None
            for k in range(KC):
                ins = e.matmul(ps[:, bl[0], 0:T], lhsT=ones32[:, :], rhs=sq[:, k, 0:T], start=(k == 0), stop=(k == KC - 1))
            return ins
        PE(fn, (sqkey, "ones32"), bkeys(bl))
        V(lambda e: e.tensor_scalar(out=rstd[:, 0:T], in0=ps[:, bl[0], 0:T], scalar1=1.0 / D, scalar2=EPS,
                                    op0=ALU.mult, op1=ALU.add), bkeys(bl), ("rstd",))
        A(lambda e: e.activation(out=rstd[:, 0:T], in_=rstd[:, 0:T], func=AF.Sqrt), ("rstd",), ("rstd",))
        V(lambda e: e.reciprocal(out=rstd[:, 0:T], in_=rstd[:, 0:T]), ("rstd",), ("rstd",))

    def pre_norm(gi, T):
        arY.reset()
        sq = arY.take([128, KC, TT], F32)
        rms_stats(xT, "xT", sq, "Y", T)
        for k in range(KC):
            V(lambda e, k=k: e.scalar_tensor_tensor(out=hT[:, k, 0:T], in0=xT[:, k, 0:T], scalar=gT[:, k, gi:gi + 1],
                                                    in1=rstd[:, 0:T], op0=ALU.mult, op1=ALU.mult),
              ("xT", "rstd", "gT"), ("hT",))
        P.barrier("Y")

    def post_norm(gi, T):
        P.barrier("S")
        arS.reset()
        sq = arS.take([128, KC, TT], F32)
        tmp = arS.take([128, 2, TT], F32)
        rms_stats(yT, "Y", sq, "S", T)
        for k in range(KC):
            V(lambda e, k=k: e.scalar_tensor_tensor(out=tmp[:, k % 2, 0:T], in0=yT[:, k, 0:T], scalar=gT[:, k, gi:gi + 1],
                                                    in1=rstd[:, 0:T], op0=ALU.mult, op1=ALU.mult),
              ("Y", "rstd", "gT"), (f"pn{k % 2}",))
            V(lambda e, k=k: e.tensor_tensor(out=xT[:, k, 0:T], in0=xT[:, k, 0:T], in1=tmp[:, k % 2, 0:T], op=ALU.add),
              (f"pn{k % 2}",), ("xT",))
        P.barrier("S")

    def ffn(l, T):
        arS.reset()
        actT = arS.take([128, FC, TT], BF16)
        sg = arS.take([128, 1, 4, TT], F32)
        ng = (FC + 3) // 4
        for g in range(ng):
            noc = min(4, FC - g * 4)

            def ev_g(oc, pap, pkey, g=g):
                A(lambda e: e.activation(out=sg[:, 0, oc, 0:T], in_=pap, func=AF.Silu), (pkey, "S"), (f"sg_{oc}",))
            linear_fm(wg, l * D, g * 512, noc * 128, hT, KC, T, ("hT",), ev_g)

            def ev_u(oc, pap, pkey, g=g):
                V(lambda e: e.tensor_tensor(out=actT[:, g * 4 + oc, 0:T], in0=sg[:, 0, oc, 0:T], in1=pap, op=ALU.mult),
                  (pkey, f"sg_{oc}", "S"), (f"act{g * 4 + oc}",))
            linear_fm(wu, l * D, g * 512, noc * 128, hT, KC, T, ("hT",), ev_u)
        akeys = tuple(f"act{c}" for c in range(FC))
        for og in range(8):
            def ev_d(oc, pap, pkey, og=og):
                if oc % 2 == 0:
                    A(lambda e: e.activation(out=yT[:, og * 4 + oc, 0:T], in_=pap, func=AF.Copy), (pkey, "Y"), (f"y{og * 4 + oc}",))
                else:
                    V(lambda e: e.tensor_copy(out=yT[:, og * 4 + oc, 0:T], in_=pap), (pkey, "Y"), (f"y{og * 4 + oc}",))
            linear_fm(wd, l * DFF, og * 512, 512, actT, FC, T, akeys + ("S",), ev_d)
        if not P.collect:
            V(lambda e: e.memset(P.scr[0:1, 1:2], 0.0), tuple(f"y{c}" for c in range(KC)), ("Y",))

    def out_proj(w, wr0, T):
        for og in range(8):
            def ev_o(oc, pap, pkey, og=og):
                if oc % 2 == 0:
                    A(lambda e: e.activation(out=yT[:, og * 4 + oc, 0:T], in_=pap, func=AF.Copy), (pkey, "Y"), (f"y{og * 4 + oc}",))
                else:
                    V(lambda e: e.tensor_copy(out=yT[:, og * 4 + oc, 0:T], in_=pap), (pkey, "Y"), (f"y{og * 4 + oc}",))
            linear_fm(w, wr0, og * 512, 512, mixT, KC, T, ("mixT",), ev_o)
        if not P.collect:
            V(lambda e: e.memset(P.scr[0:1, 1:2], 0.0), tuple(f"y{c}" for c in range(KC)), ("Y",))

    def even_mixer(l, st, T, si):
        i = l // 2
        has_s = si is not None
        arS.reset()
        uT = arS.take([128, 16, TT], BF16)
        vtm = arS.take([128, NBLK, 2048], BF16)
        vsb = arS.take([4, 2048], BF16)
        vs32 = arS.take([4, 2, 512], F32)
        zT = arS.take([128, 16, TT + 4], F32)
        gtmp = arS.take([128, 2, 512], F32)
        xbt = arS.take([128, 4, TT], F32)
        acc = arS.take([128, 2, TT + 2], F32)
        wr0 = i * D
        for j in range(4):
            def ev(oc, pap, pkey, j=j):
                c = j * 4 + oc
                gelu_evac(pap, pkey, uT[:, c, 0:T], f"u{c}", (gtmp[:, 0, 0:T], gtmp[:, 1, 0:T]), "gt")
            linear_fm(w_in, wr0, j * 512, 512, hT, KC, T, ("hT", "S"), ev)
        tbs = [(b * 128, 128) for b in range(NBLK)] + ([(TP, 4)] if has_s else [])
        for j in range(4):
            def ev(bi, pap, pkey, n, j=j):
                if bi < NBLK:
                    gelu_evac(pap, pkey, vtm[:, bi, j * 512:(j + 1) * 512], f"v{bi}", (gtmp[:, 0, :], gtmp[:, 1, :]), "gt")
                else:
                    gelu_evac(pap, pkey, vs32[0:4, j % 2, :], f"vs32{j % 2}", (gtmp[0:4, 0, :], gtmp[0:4, 1, :]), "gt",
                              extra_out=(vsb[0:4, j * 512:(j + 1) * 512], "vsb"))
                    r_ = (i * NSAMP + si) * 4
                    P.dma("sp", o_cv[r_:r_ + 4, j * 512:(j + 1) * 512], vs32[0:4, j % 2, :], reads=(f"vs32{j % 2}",), is_out=True)
            linear_tm(w_in, wr0, 2048 + j * 512, 512, hT, KC, tbs, ("hT", "S"), ev)
        if not P.collect:
            for c in range(16):
                hidx = i * 8 + c // 2
                bl = banks(1)

                def fn(e, c=c, hidx=hidx, bl=bl):
                    ins = None
                    for b in range(NBLK):
                        o = ps[:, bl[0], b * 128:(b + 1) * 128]
                        e.matmul(o, lhsT=vtm[:, b, c * 128:(c + 1) * 128], rhs=WtS[:, hidx, :], start=True, stop=False)
                        e.matmul(o, lhsT=onesb[0:1, :], rhs=bhi[0:1, hidx * 128:(hidx + 1) * 128], start=False, stop=False)
                        ins = e.matmul(o, lhsT=onesb[0:1, :], rhs=blo[0:1, hidx * 128:(hidx + 1) * 128], start=False, stop=True)
                    if has_s:
                        o = ps[:, bl[0], TP:TP + 4]
                        e.matmul(o, lhsT=vsb[0:4, c * 128:(c + 1) * 128], rhs=WtS[0:4, hidx, 0:4], start=True, stop=False)
                        e.matmul(o, lhsT=onesb[0:1, :], rhs=bhi[0:1, hidx * 128:hidx * 128 + 4], start=False, stop=False)
                        ins = e.matmul(o, lhsT=onesb[0:1, :], rhs=blo[0:1, hidx * 128:hidx * 128 + 4], start=False, stop=True)
                    return ins
                PE(fn, tuple(f"v{b}" for b in range(NBLK)) + ("vsb", "WtS", "bhl", "S"), bkeys(bl))
                V(lambda e, c=c, bl=bl: e.tensor_tensor(out=mixT[:, c, 0:T], in0=uT[:, c, 0:T], in1=ps[:, bl[0], 0:T], op=ALU.mult),
                  (f"u{c}", f"P{bl[0]}"), ("mixT",))
            A(lambda e: e.activation(out=zT[:, :, 0:2], in_=zhalo[:, i, :, :], func=AF.Copy), ("zhalo", "S"), ("zh",))
            if has_s:
                r0 = (i * NSAMP + si) * 2
                A(lambda e: e.activation(out=zT[:, :, TP + 2:TP + 4], in_=scT[:, :, r0:r0 + 2], func=AF.Copy), ("scT", "S"), ("zh2",))
        for j in range(4):
            def ev_x(oc, pap, pkey):
                A(lambda e: e.activation(out=xbt[:, oc, 0:T], in_=pap, func=AF.Copy), (pkey, "S"), (f"xb{oc}",))
            linear_fm(w_in, wr0, 4096 + j * 512, 512, hT, KC, T, ("hT",), ev_x)

            def ev_c(oc, pap, pkey, j=j):
                c = j * 4 + oc
                V(lambda e: e.tensor_tensor(out=zT[:, c, 2:2 + TP], in0=xbt[:, oc, 0:TP], in1=pap[:, 0:TP], op=ALU.mult),
                  (pkey, f"xb{oc}"), (f"z{c}",))
                if has_s:
                    V(lambda e: e.tensor_tensor(out=zT[:, c, TP + 4:TP + 8], in0=xbt[:, oc, TP:TP + 4], in1=pap[:, TP:TP + 4], op=ALU.mult),
                      (pkey, f"xb{oc}"), (f"zs{c}",))
            linear_fm(w_in, wr0, 6144 + j * 512, 512, hT, KC, T, ("hT",), ev_c)

            def ev_b(oc, pap, pkey, j=j):
                c = j * 4 + oc
                n = TP + 6 if has_s else TP
                a = acc[:, c % 2, 0:n]
                zk = (f"z{c}", f"zs{c}", "zh", "zh2")
                V(lambda e: e.tensor_scalar(out=a, in0=zT[:, c, 2:2 + n], scalar1=cwT[:, c, i * 3 + 2:i * 3 + 3], scalar2=None, op0=ALU.mult),
                  zk + ("cwT",), (f"acc{c % 2}",))
                V(lambda e: e.scalar_tensor_tensor(out=a, in0=zT[:, c, 1:1 + n], scalar=cwT[:, c, i * 3 + 1:i * 3 + 2], in1=a,
                                                   op0=ALU.mult, op1=ALU.add), zk, (f"acc{c % 2}",))
                V(lambda e: e.scalar_tensor_tensor(out=a, in0=zT[:, c, 0:n], scalar=cwT[:, c, i * 3:i * 3 + 1], in1=a,
                                                   op0=ALU.mult, op1=ALU.add), zk, (f"acc{c % 2}",))
                V(lambda e: e.tensor_tensor(out=mixT[:, 16 + c, 0:TP], in0=acc[:, c % 2, 0:TP], in1=pap[:, 0:TP], op=ALU.mult),
                  (pkey, f"acc{c % 2}"), ("mixT",))
                if has_s:
                    V(lambda e: e.tensor_tensor(out=mixT[:, 16 + c, TP:TP + 4], in0=acc[:, c % 2, TP + 2:TP + 6], in1=pap[:, TP:TP + 4], op=ALU.mult),
                      (pkey, f"acc{c % 2}"), ("mixT",))
            linear_fm(w_in, wr0, 8192 + j * 512, 512, hT, KC, T, ("hT",), ev_b)
        if not P.collect:
            allz = tuple(f"z{c}" for c in range(16)) + tuple(f"zs{c}" for c in range(16))
            V(lambda e: e.tensor_copy(out=zhalo[:, i, :, :], in_=zT[:, :, TP:TP + 2]), allz + ("zh",), ("zhalo",))

            def z_out(c0, dst):
                bl4 = banks(4)

                def fn2(e):
                    ins = None
                    for c in range(16):
                        ins = e.transpose(out=ps[0:2, bl4[c // 4], (c % 4) * 128:(c % 4 + 1) * 128], in_=zT[:, c, c0:c0 + 2], identity=ident[:, :])
                    return ins
                PE(fn2, allz + ("ident", "S"), bkeys(bl4))
                for q in range(4):
                    V(lambda e, q=q: e.tensor_copy(out=gtmp[0:2, q % 2, :], in_=ps[0:2, bl4[q], 0:512]), (f"P{bl4[q]}", "S"), (f"gt{q % 2}",))
                    P.dma("sp", dst[:, q * 512:(q + 1) * 512], gtmp[0:2, q % 2, :], reads=(f"gt{q % 2}",), is_out=True)
            if st == NST - 1:
                z_out(TP, o_cp[i * 2:i * 2 + 2, :])
            if has_s:
                z_out(TP + 6, o_cs[(i * NSAMP + si) * 2:(i * NSAMP + si) * 2 + 2, :])
        out_proj(w_out, wr0, T)

    def odd_mixer(l, st, T, si):
        i = l // 2
        has_s = si is not None
        arS.reset()
        qg = arS.take([128, 2, 2, 4, TT], BF16)
        kTb = arS.take([128, NBLK, 8, 128], BF16)
        vdb = arS.take([128, NBLK, 8, 128], BF16)
        kTc = arS.take([128, 8, 128], BF16)
        vdc = arS.take([128, 8, 128], BF16)
        kTn = arS.take([128, 8, 4], BF16)
        vdn = arS.take([4, 8, 128], BF16)
        kdup = arS.take([128, 8, 128], BF16)
        ktm = arS.take([128, 512], F32)
        vtm32 = arS.take([128, 512], F32)
        bm = arS.take([128, 8, 256], F32)
        s_sb = arS.take([128, 8, 256], F32)
        pexp2 = arS.take([128, 1, 8, 256], BF16)
        pT = arS.take([128, 8, 2, 128], BF16)
        wr0 = i * D

        KBF = _os.environ.get("DEV_KB", "")

        def k_block(src, skey, n, kT_dst, dkey, out_dma=None, from_psum=True):
            if KBF == "10":
                return
            if from_psum:
                A(lambda e: e.activation(out=ktm[0:n, :], in_=src, func=AF.Copy), (skey, "S"), ("ktm",))
            if KBF == "11":
                return
            sv = src.rearrange("p (g x) -> p g x", g=8)
            A(lambda e: e.activation(out=kdup[0:n, :, 0:64], in_=sv, func=AF.Copy), (skey, "S"), ("kdupa",))
            A(lambda e: e.activation(out=kdup[0:n, :, 64:128], in_=sv, func=AF.Copy), (skey,), ("kdupb",))
            if KBF == "12":
                return
            if out_dma is not None:
                P.dma("sp", out_dma, ktm[0:n, :], reads=("ktm",), is_out=True)
            if _os.environ.get("DEV_KB") == "1":
                return
            bl = banks(1)
            pv = psb3[:, bl[0], :].rearrange("p (g x) -> p g x", g=8)

            def fn(e):
                ins = None
                for g in range(8):
                    ins = e.transpose(out=pv[:, g, 0:n], in_=kdup[0:n, g, :], identity=identb[0:n, 0:n])
                return ins
            PE(fn, ("kdupa", "kdupb", "identb", "S"), bkeys(bl))
            if _os.environ.get("DEV_KB") == "2":
                return
            for hb in range(2):
                A(lambda e, hb=hb: e.activation(out=kT_dst[:, hb * 4:hb * 4 + 4, 0:n], in_=pv[:, hb * 4:hb * 4 + 4, 0:n], func=AF.Copy), bkeys(bl) + ("S",), (dkey,))

        def v_block(src, skey, n, vd_dst, dkey, out_dma=None, from_psum=True):
            if KBF == "10":
                return
            if from_psum:
                V(lambda e: e.tensor_copy(out=vtm32[0:n, :], in_=src), (skey, "S"), ("vtm",))
            if KBF == "11":
                return
            sv = src.rearrange("p (g x) -> p g x", g=8)
            V(lambda e: e.tensor_copy(out=vd_dst[0:n, :, 0:64], in_=sv), (skey, "S"), (dkey + "a",))
            V(lambda e: e.tensor_copy(out=vd_dst[0:n, :, 64:128], in_=sv), (skey,), (dkey + "b",))
            if KBF == "12":
                return
            if out_dma is not None:
                P.dma("sp", out_dma, vtm32[0:n, :], reads=("vtm",), is_out=True)

        tbs = [(b * 128, 128) for b in range(NBLK)] + ([(TP, 4)] if has_s else [])
        last = (st == NST - 1)
        if int(_os.environ.get("DEV_ODD", "99")) == 0:
            out_proj(w_o, wr0, T)
            return

        def ev_k(bi, pap, pkey, n):
            if bi < NBLK:
                od = o_kp[i * 128:(i + 1) * 128, :] if (last and bi == NBLK - 1) else None
                k_block(pap, pkey, n, kTb[:, bi], f"kT{bi}", od)
            else:
                r = (i * NSAMP + si) * 128
                k_block(pap, pkey, n, kTn, "kTn", o_ks[r + 124:r + 128, :])
        linear_tm(w_qkv, wr0, 4096, 512, hT, KC, tbs, ("hT", "S"), ev_k)

        def ev_v(bi, pap, pkey, n):
            if bi < NBLK:
                od = o_vp[i * 128:(i + 1) * 128, :] if (last and bi == NBLK - 1) else None
                v_block(pap, pkey, n, vdb[:, bi], f"vd{bi}", od)
            else:
                r = (i * NSAMP + si) * 128
                v_block(pap, pkey, n, vdn, "vdn", o_vs[r + 124:r + 128, :])
        linear_tm(w_qkv, wr0, 4608, 512, hT, KC, tbs, ("hT", "S"), ev_v)
        ODD = int(_os.environ.get("DEV_ODD", "99"))
        if ODD == 1:
            out_proj(w_o, wr0, T)
            return
        if has_s and not P.collect:
            r = (i * NSAMP + si) * 128
            P.dma("sp", ktm[:, :], cwk[r:r + 128, :], reads=("S",), writes=("ktm",))
            P.dma("sp", o_ks[r:r + 124, :], ktm[4:128, :], reads=("ktm",), is_out=True)
            k_block(ktm[:, :], "ktm", 128, kTc, "kTc", None, from_psum=False)
            P.dma("sp", vtm32[:, :], cwv[r:r + 128, :], reads=("S",), writes=("vtm",))
            P.dma("sp", o_vs[r:r + 124, :], vtm32[4:128, :], reads=("vtm",), is_out=True)
            v_block(vtm32[:, :], "vtm", 128, vdc, "vdc", None, from_psum=False)
        if ODD == 2:
            out_proj(w_o, wr0, T)
            return
        if not P.collect:
            V(lambda e: e.memset(qg[64:128, :, 0], 0.0), ("S",), ("qz",))
            V(lambda e: e.memset(qg[0:64, :, 1], 0.0), ("S",), ("qz",))
        qblocks = []
        for b in range(NBLK):
            if b == 0:
                prev = (kTh[:, i], vdh[:, i], 128, ("kTh",), ("vdh",))
            else:
                prev = (kTb[:, b - 1], vdb[:, b - 1], 128, (f"kT{b - 1}",), (f"vd{b - 1}a", f"vd{b - 1}b"))
            own = (kTb[:, b], vdb[:, b], 128, (f"kT{b}",), (f"vd{b}a", f"vd{b}b"))
            qblocks.append((b * 128, 128, [prev, own], (st == 0 and b == 0)))
        if has_s:
            qblocks.append((TP, 4, [(kTc, vdc, 128, ("kTc",), ("vdca", "vdcb")), (kTn, vdn, 4, ("kTn",), ("vdna", "vdnb"))], False))
        def stage1(u, g, q0, QB, segs, first):
            par = 0
            NK = 128 + segs[1][2]
            bs = banks(4)
            psv = ps[:, bs[0]:bs[0] + 4, :].rearrange("p b (h k) -> p (b h) k", h=2)

            def fqk(e):
                ins = None
                for ii in range(8):
                    off = 0
                    for (kt, vd, n, kk, vk) in segs:
                        ins = e.matmul(psv[0:QB, ii, off:off + n], lhsT=qg[:, g % 2, ii % 2, ii // 2, q0:q0 + QB], rhs=kt[:, g, 0:n],
                                       start=True, stop=True)
                        off += n
                return ins
            PE(fqk, tuple(f"ql{g % 2}_{x}" for x in range(4)) + tuple(f"qh{g % 2}_{x}" for x in range(4)) + ("qz",) + segs[0][3] + segs[1][3] + ("S",), bkeys(bs))
            V(lambda e: e.tensor_tensor(out=s_sb[0:QB, :, 0:NK], in0=psv[0:QB, :, 0:NK], in1=bm[0:QB, :, 0:NK], op=ALU.add),
              bkeys(bs) + ("bm", "S"), ("s_sb",))
            if first:
                for ii in range(8):
                    V(lambda e, ii=ii: e.tensor_tensor(out=s_sb[:, ii, :], in0=s_sb[:, ii, :], in1=fmask[:, :], op=ALU.add), ("fmask",), ("s_sb",))
            mx, negm, sm, rs, es, den = (small[0:QB, 0:8], small[0:QB, 8:16], small[0:QB, 16:24], small[0:QB, 24:32],
                                         small[0:QB, 32:40], small[0:QB, 40:48])
            skc = sinkbc[0:QB, i * 64 + g * 8:i * 64 + g * 8 + 8]
            rsk = tuple(f"rs{ii}" for ii in range(8))
            V(lambda e: e.tensor_reduce(out=mx, in_=s_sb[0:QB, :, 0:NK], axis=AX.X, op=ALU.max), ("s_sb",), ("mx",))
            V(lambda e: e.tensor_tensor(out=mx, in0=mx, in1=skc, op=ALU.max), ("sinkbc",), ("mx",))
            V(lambda e: e.tensor_scalar(out=negm, in0=mx, scalar1=-1.0, scalar2=None, op0=ALU.mult), ("mx",), ("negm",))
            V(lambda e: e.tensor_tensor(out=sm, in0=skc, in1=mx, op=ALU.subtract), ("mx", "sinkbc"), ("sm",))
            V(lambda e: e.memset(rs, 0.0), (), rsk)

        def stage1b(u, g, q0, QB, segs, first):
            par = 0
            NK = 128 + segs[1][2]
            mx, negm, sm, rs, es, den = (small[0:QB, 0:8], small[0:QB, 8:16], small[0:QB, 16:24], small[0:QB, 24:32],
                                         small[0:QB, 32:40], small[0:QB, 40:48])
            rsk = tuple(f"rs{ii}" for ii in range(8))
            for ii in range(8):
                A(lambda e, ii=ii: e.activation(out=pexp2[0:QB, par, ii, 0:NK], in_=s_sb[0:QB, ii, 0:NK], func=AF.Exp,
                                                bias=negm[:, ii:ii + 1], scale=1.0, accum_out=rs[:, ii:ii + 1]),
                  ("s_sb", "negm"), (f"pe{par}_{ii}", f"rs{ii}"))
            A(lambda e: e.activation(out=es, in_=sm, func=AF.Exp), ("sm",), ("es",))
            V(lambda e: e.tensor_tensor(out=den, in0=rs, in1=es, op=ALU.add), rsk + ("es",), ("den",))
            V(lambda e: e.reciprocal(out=den, in_=den), (), ("den",))
            for ii in range(8):
                if ii < 6:
                    V(lambda e, ii=ii: e.tensor_scalar(out=pexp2[0:QB, par, ii, 0:NK], in0=pexp2[0:QB, par, ii, 0:NK], scalar1=den[:, ii:ii + 1],
                                                       scalar2=None, op0=ALU.mult), ("den",), (f"pe{par}_{ii}",))
                else:
                    A(lambda e, ii=ii: e.activation(out=pexp2[0:QB, par, ii, 0:NK], in_=pexp2[0:QB, par, ii, 0:NK], func=AF.Copy,
                                                    scale=den[:, ii:ii + 1]), ("den",), (f"pe{par}_{ii}",))

        def stage2(u, g, q0, QB, segs):
            par = 0
            n0, n1 = segs[0][2], segs[1][2]
            bt = banks(2)
            ptv = psb3[:, bt[0]:bt[0] + 2, :].rearrange("p b (h s q) -> p (b h) s q", h=4, s=2)

            def ftr(e):
                ins = None
                for ii in range(8):
                    off = 0
                    for sx, (kt, vd, n, kk, vk) in enumerate(segs):
                        ins = e.transpose(out=ptv[0:n, ii, sx, 0:QB], in_=pexp2[0:QB, par, ii, off:off + n], identity=identb[0:QB, 0:QB])
                        off += n
                return ins
            PE(ftr, tuple(f"pe{par}_{ii}" for ii in range(8)) + ("identb", "S"), bkeys(bt))
            for hf, opf in ((0, A), (1, V)):
                for sx, nn in ((0, n0), (1, n1)):
                    if opf is A:
                        A(lambda e, hf=hf, sx=sx, nn=nn: e.activation(out=pT[0:nn, hf * 4:hf * 4 + 4, sx, 0:QB], in_=ptv[0:nn, hf * 4:hf * 4 + 4, sx, 0:QB], func=AF.Copy),
                          (f"P{bt[hf]}", "S"), (f"pT{sx}{hf}",))
                    else:
                        V(lambda e, hf=hf, sx=sx, nn=nn: e.tensor_copy(out=pT[0:nn, hf * 4:hf * 4 + 4, sx, 0:QB], in_=ptv[0:nn, hf * 4:hf * 4 + 4, sx, 0:QB]),
                          (f"P{bt[hf]}", "S"), (f"pT{sx}{hf}",))
            bo = banks(2)
            pov = ps[:, bo[0]:bo[0] + 2, :].rearrange("p b (h q) -> p (b h) q", h=4)

            def fpv(e):
                ins = None
                for ii in range(8):
                    for sx, (kt, vd, n, kk, vk) in enumerate(segs):
                        ins = e.matmul(pov[:, ii, 0:QB], lhsT=vd[0:n, g, :], rhs=pT[0:n, ii, sx, 0:QB], start=(sx == 0), stop=(sx == 1))
                return ins
            PE(fpv, ("pT00", "pT01", "pT10", "pT11", "S") + segs[0][4] + segs[1][4], bkeys(bo))
            pe4 = pov.rearrange("p (a two) q -> p a two q", two=2)
            A(lambda e: e.activation(out=mixT[0:64, g * 4:g * 4 + 2, q0:q0 + QB], in_=pe4[0:64, 0:2, 0, 0:QB], func=AF.Copy), (f"P{bo[0]}",), ("mixT",))
            A(lambda e: e.activation(out=mixT[64:128, g * 4:g * 4 + 2, q0:q0 + QB], in_=pe4[64:128, 0:2, 1, 0:QB], func=AF.Copy), (f"P{bo[0]}",), ("mixT",))
            V(lambda e: e.tensor_copy(out=mixT[0:64, g * 4 + 2:g * 4 + 4, q0:q0 + QB], in_=pe4[0:64, 2:4, 0, 0:QB]), (f"P{bo[1]}",), ("mixT",))
            V(lambda e: e.tensor_copy(out=mixT[64:128, g * 4 + 2:g * 4 + 4, q0:q0 + QB], in_=pe4[64:128, 2:4, 1, 0:QB]), (f"P{bo[1]}",), ("mixT",))

        pend = None
        ucnt = 0
        for g in range(8):
            def ev_q(oc, pap, pkey, g=g):
                if oc % 2 == 0:
                    A(lambda e: e.activation(out=qg[0:64, g % 2, 0, oc, 0:T], in_=pap[0:64, :], func=AF.Copy, scale=0.125), (pkey, "S", "qz"), (f"ql{g % 2}_{oc}",))
                    A(lambda e: e.activation(out=qg[64:128, g % 2, 1, oc, 0:T], in_=pap[64:128, :], func=AF.Copy, scale=0.125), (pkey, "S", "qz"), (f"qh{g % 2}_{oc}",))
                else:
                    V(lambda e: e.tensor_scalar(out=qg[0:64, g % 2, 0, oc, 0:T], in0=pap[0:64, :], scalar1=0.125, scalar2=None, op0=ALU.mult),
                      (pkey, "S", "qz"), (f"ql{g % 2}_{oc}",))
                    V(lambda e: e.tensor_scalar(out=qg[64:128, g % 2, 1, oc, 0:T], in0=pap[64:128, :], scalar1=0.125, scalar2=None, op0=ALU.mult),
                      (pkey, "S", "qz"), (f"qh{g % 2}_{oc}",))
            linear_fm(w_qkv, wr0, g * 512, 512, hT, KC, T, ("hT",), ev_q)
            if P.collect:
                continue
            P.dma("sp", bm.rearrange("p a b -> p (a b)"), bm_dram[g * 128:(g + 1) * 128, :], reads=("S",), writes=("bm",))
            for (q0, QB, segs, first) in qblocks:
                stage1(ucnt, g, q0, QB, segs, first)
                if pend is not None:
                    stage2(*pend)
                stage1b(ucnt, g, q0, QB, segs, first)
                pend = (ucnt, g, q0, QB, segs)
                ucnt += 1
        if pend is not None:
            stage2(*pend)
        if not P.collect:
            lb = NBLK - 1
            V(lambda e: e.tensor_copy(out=kTh[:, i], in_=kTb[:, lb]), (f"kT{lb}", "S"), ("kTh",))
            A(lambda e: e.activation(out=vdh[:, i], in_=vdb[:, lb], func=AF.Copy), (f"vd{lb}a", f"vd{lb}b", "S"), ("vdh",))
        out_proj(w_o, wr0, T)

    def setup():
        if P.collect:
            return
        arS.reset()
        cS = arS.take([128, 1280], F32)
        P.dma("sp", cS, cst[:, :], writes=("cS",))
        V(lambda e: e.tensor_copy(out=ident[:, :], in_=cS[:, 0:128]), ("cS",), ("ident",))
        V(lambda e: e.tensor_copy(out=identb[:, :], in_=cS[:, 0:128]), ("cS",), ("identb",))
        V(lambda e: e.tensor_copy(out=fmask[:, :], in_=cS[:, 895:1151]), ("cS",), ("fmask",))
        V(lambda e: e.memset(ones32[:, :], 1.0), (), ("ones32",))
        V(lambda e: e.memset(onesb[:, :], 1.0), (), ("bhl",))
        V(lambda e: e.memset(zhalo[:, :, :, :], 0.0), (), ("zhalo",))
        V(lambda e: e.memset(kTh[:, :, :, :], 0.0), (), ("kTh",))
        V(lambda e: e.memset(vdh[:, :, :, :], 0.0), (), ("vdh",))
        tok = arS.take([16, 4096], F32)

        def tr_rows(src_dram, nrows, ncols, dst_fn, name):
            P.dma("sp", tok[0:nrows, 0:ncols], src_dram, reads=(), writes=("tok",))
            nch = ncols // 128
            for c0 in range(0, nch, 16):
                bl = banks(1)

                def fn(e, c0=c0, bl=bl):
                    ins = None
                    for c in range(c0, min(nch, c0 + 16)):
                        ins = e.transpose(out=ps[:, bl[0], (c - c0) * 16:(c - c0) * 16 + nrows], in_=tok[0:nrows, c * 128:(c + 1) * 128],
                                          identity=ident[0:nrows, 0:nrows])
                    return ins
                PE(fn, ("tok", "ident"), bkeys(bl))
                for c in range(c0, min(nch, c0 + 16)):
                    V(lambda e, c=c, c0=c0, bl=bl: e.tensor_copy(out=dst_fn(c), in_=ps[:, bl[0], (c - c0) * 16:(c - c0) * 16 + nrows]),
                      bkeys(bl), (name,))
        tr_rows(gall[:, :], 16, 4096, lambda c: gT[:, c, :], "gT")
        tr_rows(convw[:, :], NE * 3, 2048, lambda c: cwT[:, c, 0:NE * 3], "cwT")
        tr_rows(scv[:, :], NE * NSAMP * 2, 2048, lambda c: scT[:, c, 0:NE * NSAMP * 2], "scT")
        wtmp = arS.take([128, NE * 8, 128], F32)
        P.dma("sp", wtmp, sguw.rearrange("(a i) j -> i a j", i=128), writes=("wtmp",))
        for a in range(NE * 8):
            bl = banks(1)
            PE(lambda e, a=a, bl=bl: e.transpose(out=ps[:, bl[0], 0:128], in_=wtmp[:, a, :], identity=ident[:, :]), ("wtmp", "ident"), bkeys(bl))
            V(lambda e, a=a, bl=bl: e.tensor_tensor(out=WtS[:, a, :], in0=ps[:, bl[0], 0:128], in1=cS[:, 128:256], op=ALU.mult), bkeys(bl) + ("cS",), ("WtS",))
        b32 = arS.take([1, NE * 8 * 128], F32)
        b32b = arS.take([1, NE * 8 * 128], F32)
        P.dma("sp", b32, sgub[:, :], writes=("b32",))
        V(lambda e: e.tensor_copy(out=bhi[:, :], in_=b32), ("b32",), ("bhl",))
        V(lambda e: e.tensor_copy(out=b32b, in_=bhi[:, :]), ("bhl",), ("b32b",))
        V(lambda e: e.tensor_tensor(out=b32b, in0=b32, in1=b32b, op=ALU.subtract), ("b32",), ("b32b",))
        V(lambda e: e.tensor_copy(out=blo[:, :], in_=b32b), ("b32b",), ("bhl",))
        if NO > 0:
            srow = arS.take([1, NOD * 64], F32)
            P.dma("sp", srow, sinks[:, :], writes=("srow",))
            bl = banks(1)
            PE(lambda e, bl=bl: e.matmul(ps[:, bl[0], 0:NOD * 64], lhsT=ones32[0:1, :], rhs=srow, start=True, stop=True), ("srow", "ones32"), bkeys(bl))
            V(lambda e, bl=bl: e.tensor_copy(out=sinkbc[:, :], in_=ps[:, bl[0], 0:NOD * 64]), bkeys(bl), ("sinkbc",))
            rb = arS.take([32, 64], F32)
            fdh = arS.take([128, 64], F32)
            P.dma("sp", rb, relb[:, :], writes=("rb",))
            bl = banks(1)
            PE(lambda e, bl=bl: e.matmul(ps[:, bl[0], 0:64], lhsT=cS[0:32, 1151:1279], rhs=rb, start=True, stop=True), ("rb", "cS"), bkeys(bl))
            V(lambda e, bl=bl: e.tensor_copy(out=fdh, in_=ps[:, bl[0], 0:64]), bkeys(bl), ("fdh",))
            bmg = arS.take([128, 1, 8, 256], F32)
            for g in range(8):
                bl = banks(4)
                pv = ps[:, bl[0]:bl[0] + 4, :].rearrange("p b (k h) -> p (b k) h", h=8)

                def fn(e, g=g, pv=pv):
                    ins = None
                    for k in range(256):
                        ins = e.matmul(pv[:, k, :], lhsT=cS[:, 256 + 255 - k:256 + 255 - k + 128], rhs=fdh[:, g * 8:(g + 1) * 8], start=True, stop=True)
                    return ins
                PE(fn, ("cS", "fdh"), bkeys(bl))
                for ii in range(8):
                    V(lambda e, g=g, ii=ii, pv=pv: e.tensor_tensor(out=bmg[:, 0, ii, :], in0=pv[:, :, ii], in1=cS[:, 639:895], op=ALU.add),
                      bkeys(bl) + ("cS",), ("bmg0",))
                P.dma("sp", bm_dram[g * 128:(g + 1) * 128, :], bmg[:, 0].rearrange("p a b -> p (a b)"), reads=("bmg0",), writes=("bmdram",))
        P.barrier("S")
        P.op("sp", lambda e: e.dma_start(out=P.scr[0:1, 2:3], in_=cst[0:1, 0:1]), ("bmdram",), ("S2",), inc=False) if False else None

    def load_x(st, T, si):
        if P.collect:
            return
        arS.reset()
        tok = arS.take([128, 2, 4096], F32)
        srcs = [(xp[st * TP + b * 128: st * TP + (b + 1) * 128, :], 128, b * 128) for b in range(NBLK)]
        if si is not None:
            srcs.append((xs[si * 4:si * 4 + 4, :], 4, TP))
        for n_, (src, n, c0) in enumerate(srcs):
            d = n_ % 2
            P.dma("sp", tok[0:n, d, :], src, reads=("S",), writes=(f"tok{d}",))
            for k0 in range(0, KC, 4):
                bl = banks(1)

                def fn(e, k0=k0, bl=bl, n=n, d=d):
                    ins = None
                    for k in range(k0, k0 + 4):
                        ins = e.transpose(out=ps[:, bl[0], (k - k0) * 128:(k - k0) * 128 + n], in_=tok[0:n, d, k * 128:(k + 1) * 128], identity=ident[0:n, 0:n])
                    return ins
                PE(fn, (f"tok{d}", "ident"), bkeys(bl))
                pv = ps[:, bl[0], :].rearrange("p (k t) -> p k t", k=4)
                if (k0 // 4) % 2 == 0:
                    V(lambda e, k0=k0, pv=pv, n=n, c0=c0: e.tensor_copy(out=xT[:, k0:k0 + 4, c0:c0 + n], in_=pv[:, :, 0:n]), bkeys(bl), ("xT",))
                else:
                    A(lambda e, k0=k0, pv=pv, n=n, c0=c0: e.activation(out=xT[:, k0:k0 + 4, c0:c0 + n], in_=pv[:, :, 0:n], func=AF.Copy), bkeys(bl), ("xT",))
        P.barrier("S")

    def store_x(st, T, si):
        if P.collect:
            return
        P.barrier("S")
        arS.reset()
        tok = arS.take([128, 2, 4096], F32)
        dsts = [(o_yp[st * TP + b * 128: st * TP + (b + 1) * 128, :], 128, b * 128) for b in range(NBLK)]
        if si is not None:
            dsts.append((o_ys[si * 4:si * 4 + 4, :], 4, TP))
        for n_, (dst, n, c0) in enumerate(dsts):
            d = n_ % 2
            for k0 in range(0, KC, 4):
                bl = banks(1)

                def fn(e, k0=k0, bl=bl, n=n, c0=c0):
                    ins = None
                    for k in range(k0, k0 + 4):
                        ins = e.transpose(out=ps[0:n, bl[0], (k - k0) * 128:(k - k0 + 1) * 128], in_=xT[:, k, c0:c0 + n], identity=ident[:, :])
                    return ins
                PE(fn, ("xT", "ident"), bkeys(bl))
                if (k0 // 4) % 2 == 0:
                    V(lambda e, k0=k0, bl=bl, n=n, d=d: e.tensor_copy(out=tok[0:n, d, k0 * 128:(k0 + 4) * 128], in_=ps[0:n, bl[0], :]), bkeys(bl) + ("S",), (f"tok{d}",))
                else:
                    A(lambda e, k0=k0, bl=bl, n=n, d=d: e.activation(out=tok[0:n, d, k0 * 128:(k0 + 4) * 128], in_=ps[0:n, bl[0], :], func=AF.Copy),
                      bkeys(bl) + ("S",), (f"tok{d}",))
            P.dma("sp", dst, tok[0:n, d, :], reads=(f"tok{d}",), is_out=True)
        P.barrier("S")

    STOP = int(_os.environ.get("DEV_STOP", "99"))

    def program():
        if STOP >= 0:
            setup()
        for st in range(NST):
            si = st if st < NSAMP else None
            T = TP + (4 if si is not None else 0)
            load_x(st, T, si)
            for l in range(DEPTH if STOP >= 2 else 0):
                pre_norm(0 * 4 + l, T)
                if STOP == 2:
                    continue
                if STOP == 6 and l == 0:
                    continue
                if l % 2 == 0:
                    even_mixer(l, st, T, si)
                else:
                    odd_mixer(l, st, T, si)
                if STOP == 3 or STOP == 6:
                    break
                post_norm(1 * 4 + l, T)
                if STOP == 4:
                    break
                pre_norm(2 * 4 + l, T)
                ffn(l, T)
                post_norm(3 * 4 + l, T)
                if STOP == 5:
                    break
            store_x(st, T, si)

    P.collect = True
    program()
    P.collect = False
    bank_rr[0] = 0
    assert len(jobs) % NST == 0
    wbf[1] = len(jobs) // NST
    for j_, jb in enumerate(jobs):
        assert jb[1:] == jobs[j_ % wbf[1]][1:], "per-step job sequence differs"
    wbf[0] = [nc.dram_tensor(f"wbf{t_}", [200 * 128, 4096], BF16, kind="Internal").ap() for t_ in range((wbf[1] + 199) // 200)]
    with nc.Block() as block:
        program()
        P.emit(block)
    return nc


def host_inputs(cfg, x_prompt_core, x_sample_core, state_conv_core, ck_core, cv_core, W):
    m = dict(W)
    m["xp"] = np.ascontiguousarray(x_prompt_core.reshape(-1, D))
    m["xs"] = np.ascontiguousarray(x_sample_core.reshape(-1, D))
    m["scv"] = np.ascontiguousarray(state_conv_core.reshape(-1, 2048))
    m["cwk"] = np.ascontiguousarray(ck_core.reshape(-1, 512))
    m["cwv"] = np.ascontiguousarray(cv_core.reshape(-1, 512))
    return m


def shared_weights(cfg, norm_mix_pre, norm_mix_post, norm_ffn_pre, norm_ffn_post, w_in_even, w_out_even, sgu_w, sgu_b,
                   conv_w, w_qkv_odd, w_o_odd, attn_sinks, rel_bias, ffn_w_gate, ffn_w_up, ffn_w_down):
    f = lambda a: np.ascontiguousarray(np.asarray(a, dtype=np.float32))
    dp = cfg.depth
    g = np.zeros((16, D), np.float32)
    for t, a in enumerate((norm_mix_pre, norm_mix_post, norm_ffn_pre, norm_ffn_post)):
        g[t * 4:t * 4 + dp] = np.asarray(a)[:dp]
    W = {
        "gall": g,
        "w_in": f(w_in_even).reshape(-1, 10240),
        "w_out": f(w_out_even).reshape(-1, D),
        "sguw": f(sgu_w).reshape(-1, 128),
        "sgub": f(sgu_b).reshape(1, -1),
        "convw": f(conv_w).reshape(-1, 2048),
        "w_qkv": f(w_qkv_odd).reshape(-1, 5120),
        "w_o": f(w_o_odd).reshape(-1, D),
        "sinks": f(attn_sinks).reshape(1, -1),
        "relb": f(rel_bias),
        "wg": f(ffn_w_gate).reshape(-1, cfg.dff),
        "wu": f(ffn_w_up).reshape(-1, cfg.dff),
        "wd": f(ffn_w_down).reshape(-1, D),
        "cst": make_consts(),
    }
    return W


def kernel(x_prompt, x_sample, state_conv, cache_win_k, cache_win_v,
           norm_mix_pre, norm_mix_post, norm_ffn_pre, norm_ffn_post,
           w_in_even, w_out_even, sgu_w, sgu_b, conv_w,
           w_qkv_odd, w_o_odd, attn_sinks, rel_bias,
           ffn_w_gate, ffn_w_up, ffn_w_down):
    cfg = Cfg()
    x_prompt = np.asarray(x_prompt, np.float32)
    x_sample = np.asarray(x_sample, np.float32)
    state_conv = np.asarray(state_conv, np.float32)
    cache_win_k = np.asarray(cache_win_k, np.float32)
    cache_win_v = np.asarray(cache_win_v, np.float32)
    W = shared_weights(cfg, norm_mix_pre, norm_mix_post, norm_ffn_pre, norm_ffn_post, w_in_even, w_out_even, sgu_w, sgu_b,
                       conv_w, w_qkv_odd, w_o_odd, attn_sinks, rel_bias, ffn_w_gate, ffn_w_up, ffn_w_down)
    nc = build(cfg)
    in_maps = []
    for c in range(8):
        s, hh = c // 2, c % 2
        b0 = 0 if hh == 0 else 6
        xpc = x_prompt[s, b0 * 128:(b0 + 10) * 128]
        sl = slice(4 * c, 4 * c + 4)
        in_maps.append(host_inputs(cfg, xpc, x_sample[sl], state_conv[:, sl], cache_win_k[:, sl].reshape(2, 4, 128, 512),
                                   cache_win_v[:, sl].reshape(2, 4, 128, 512), W))
    res = run_bass_kernel_spmd(nc, in_maps, core_ids=list(range(8))).results
    B, S = 4, 2048
    y_p = np.zeros((B, S, D), np.float32)
    y_s = np.zeros((32, 4, D), np.float32)
    conv_p = np.zeros((2, B, 2, 2048), np.float32)
    conv_s = np.zeros((2, 32, 2, 2048), np.float32)
    kp = np.zeros((2, B, 128, 8, 64), np.float32)
    vp = np.zeros((2, B, 128, 8, 64), np.float32)
    ks = np.zeros((2, 32, 128, 8, 64), np.float32)
    vs = np.zeros((2, 32, 128, 8, 64), np.float32)
    cv = np.zeros((2, 32, 4, 8, 256), np.float32)
    for c in range(8):
        r = res[c]
        s, hh = c // 2, c % 2
        yp = r["o_yp"].reshape(10 * 128, D)
        if hh == 0:
            y_p[s, 0:1280] = yp
        else:
            y_p[s, 1280:2048] = yp[4 * 128:]
            conv_p[:, s] = r["o_cp"].reshape(2, 2, 2048)
            kp[:, s] = r["o_kp"].reshape(2, 128, 8, 64)
            vp[:, s] = r["o_vp"].reshape(2, 128, 8, 64)
        sl = slice(4 * c, 4 * c + 4)
        y_s[sl] = r["o_ys"].reshape(4, 4, D)
        conv_s[:, sl] = r["o_cs"].reshape(2, 4, 2, 2048)
        ks[:, sl] = r["o_ks"].reshape(2, 4, 128, 8, 64)
        vs[:, sl] = r["o_vs"].reshape(2, 4, 128, 8, 64)
        cv[:, sl] = r["o_cv"].reshape(2, 4, 4, 8, 256)
    return (y_p, y_s, conv_p, conv_s, kp, vp, ks, vs, cv)
```

```python
import math
import numpy as np
import ml_dtypes
import concourse.bass as bass
import concourse.mybir as mybir
from concourse.bass_utils import run_bass_kernel_spmd

F32 = mybir.dt.float32
BF16 = mybir.dt.bfloat16
AF = mybir.ActivationFunctionType
ALU = mybir.AluOpType
AX = mybir.AxisListType

D = 4096
KC = 32
EPS = 1e-6
NEG = -1e30
SEM_LIMIT = 12000
NSLOT = 3
USE_WCACHE = True


class Sem:
    def __init__(self, h, idx, eng):
        self.h, self.idx, self.eng = h, idx, eng


class Plan:
    ENGS = ("sp", "pool", "dve", "act", "pe")

    def __init__(self, nc):
        self.nc = nc
        self.lists = {e: [] for e in self.ENGS}
        self.nsem = 0
        self.cnt = {}
        self.buf = {}
        self.waited = {e: {} for e in self.ENGS}
        self.collect = False
        self.dma_pool = []
        self.dma_rr = 0
        self.out_events = []

    def new_sem(self, eng):
        h = self.nc.alloc_semaphore(f"s{self.nsem}")
        s = Sem(h, self.nsem, eng)
        self.nsem += 1
        return s

    def next_ev(self, eng):
        c = self.cnt.get(eng)
        if c is None or c[1] >= SEM_LIMIT:
            c = [self.new_sem(eng), 0]
            self.cnt[eng] = c
        c[1] += 1
        return (c[0], c[1])

    def _deps(self, reads, writes):
        d = []
        for k in reads:
            b = self.buf.get(k)
            if b and b[0] is not None:
                d.append(b[0])
        for k in writes:
            b = self.buf.get(k)
            if b:
                if b[0] is not None:
                    d.append(b[0])
                d.extend(b[1].values())
        return d

    def _commit(self, ev, reads, writes):
        for k in reads:
            b = self.buf.setdefault(k, [None, {}])
            o = b[1].get(ev[0].idx)
            if o is None or o[1] < ev[1]:
                b[1][ev[0].idx] = ev
        for k in writes:
            self.buf[k] = [ev, {}]

    def _waits(self, eng, deps):
        for (s, v) in deps:
            if s.eng == "pe" and eng == "pe":
                continue
            if self.waited[eng].get(s.idx, 0) >= v:
                continue
            self.waited[eng][s.idx] = v
            self.lists[eng].append(("w", s, v))

    def op(self, eng, fn, reads=(), writes=(), inc=True):
        if self.collect:
            return None
        if "S" not in writes and "S" not in reads:
            reads = tuple(reads) + ("S",)
        if eng in ("act", "dve"):
            pk = tuple(k for k in reads if len(k) == 2 and k[0] == "P" and k[1].isdigit())
            if pk:
                reads = tuple(k for k in reads if k not in pk)
                writes = tuple(writes) + pk
        self._waits(eng, self._deps(reads, writes))
        ev = self.next_ev(eng) if inc else None
        self.lists[eng].append(("o", fn, ev, 1))
        if ev is not None:
            self._commit(ev, reads, writes)
        return ev

    def dma(self, q, out, in_, reads=(), writes=(), sem=None, is_out=False, noauto=False):
        if self.collect:
            return None
        if sem is None and "S" not in reads and not noauto:
            reads = tuple(reads) + ("S",)
        if sem is None:
            k = self.dma_rr % 24
            if k >= len(self.dma_pool):
                self.dma_pool.append([self.new_sem(None), 0])
            ent = self.dma_pool[k]
            self.dma_rr += 1
        else:
            ent = sem
        deps = self._deps(reads, writes)
        if ent[1] > 0:
            deps.append((ent[0], ent[1]))
        self._waits(q, deps)
        ent[1] += 16
        ev = (ent[0], ent[1])
        self.lists[q].append(("o", lambda e, o=out, i=in_: e.dma_start(out=o, in_=i), ev, 16))
        self._commit(ev, reads, writes)
        if is_out:
            self.out_events.append(ev)
        return ev

    def barrier(self, key):
        self.op("dve", lambda e: e.memset(self.scr[0:1, 0:1], 0.0), reads=(), writes=(key,))

    def emit(self, block):
        names = {"sp": "sync", "pool": "gpsimd", "dve": "vector", "act": "scalar", "pe": "tensor"}
        last = {}
        for (s, v) in self.out_events:
            if last.get(s.idx, (None, 0))[1] < v:
                last[s.idx] = (s, v)
        for (s, v) in last.values():
            self.lists["sp"].append(("w", s, v))
        for en in self.ENGS:
            items = self.lists[en]

            def body(eng, items=items):
                for it in items:
                    if it[0] == "w":
                        eng.wait_ge(it[1].h, it[2])
                    else:
                        ins = it[1](eng)
                        if it[2] is not None:
                            ins.then_inc(it[2][0].h, it[3])
            getattr(block, names[en])(body)


def t5_bucket_np(dist):
    max_exact = 16
    d = np.maximum(dist, max_exact).astype(np.float32)
    large = max_exact + (np.log(d / max_exact) / math.log(128 / max_exact) * (32 - max_exact)).astype(np.int32)
    return np.where(dist < max_exact, dist, np.minimum(large, 31))


def make_consts():
    c = np.zeros((128, 1280), np.float32)
    c[:, 0:128] = np.eye(128, dtype=np.float32)
    j = np.arange(128)[:, None]
    i = np.arange(128)[None, :]
    c[:, 128:256] = (j <= i).astype(np.float32)
    cc = np.arange(383)[None, :]
    c[:, 256:639] = (cc == (np.arange(128)[:, None] + 127)).astype(np.float32)
    q = np.arange(128)[:, None]
    k = np.arange(256)[None, :]
    dist = q + 128 - k
    valid = (dist >= 0) & (dist < 128)
    c[:, 639:895] = np.where(valid, 0.0, NEG).astype(np.float32)
    fm = np.zeros((128, 256), np.float32)
    fm[:, 0:128] = NEG
    c[:, 895:1151] = fm
    bk = t5_bucket_np(np.arange(128))
    oh = np.zeros((32, 128), np.float32)
    oh[bk, np.arange(128)] = 1.0
    c[0:32, 1151:1279] = oh
    return c


class Cfg:
    def __init__(self, nstep=5, depth=4, dff=11008, nblk=2, nsamp=4):
        self.nstep, self.depth, self.dff, self.nblk, self.nsamp = nstep, depth, dff, nblk, nsamp
        self.n_even = (depth + 1) // 2
        self.n_odd = depth // 2
        self.fc = dff // 128
        self.tp = nblk * 128
        self.t = self.tp + 4


def build(cfg):
    import os as _os
    nc = bass.Bass("TRN2", target_bir_lowering=False)
    P = Plan(nc)
    NST, DEPTH, DFF, NBLK, NSAMP = cfg.nstep, cfg.depth, cfg.dff, cfg.nblk, cfg.nsamp
    NE, NO, FC, TP, TT = cfg.n_even, cfg.n_odd, cfg.fc, cfg.tp, cfg.t
    NOD = max(NO, 1)

    def din(name, shape):
        return nc.dram_tensor(name, list(shape), F32, kind="ExternalInput").ap()

    def dout(name, shape):
        return nc.dram_tensor(name, list(shape), F32, kind="ExternalOutput").ap()

    xp = din("xp", [NST * TP, D])
    xs = din("xs", [NSAMP * 4, D])
    scv = din("scv", [NE * NSAMP * 2, 2048])
    cwk = din("cwk", [NOD * NSAMP * 128, 512])
    cwv = din("cwv", [NOD * NSAMP * 128, 512])
    gall = din("gall", [16, D])
    w_in = din("w_in", [NE * D, 10240])
    w_out = din("w_out", [NE * D, D])
    sguw = din("sguw", [NE * 8 * 128, 128])
    sgub = din("sgub", [1, NE * 8 * 128])
    convw = din("convw", [NE * 3, 2048])
    w_qkv = din("w_qkv", [NOD * D, 5120])
    w_o = din("w_o", [NOD * D, D])
    sinks = din("sinks", [1, NOD * 64])
    relb = din("relb", [32, 64])
    wg = din("wg", [DEPTH * D, DFF])
    wu = din("wu", [DEPTH * D, DFF])
    wd = din("wd", [DEPTH * DFF, D])
    cst = din("cst", [128, 1280])

    o_yp = dout("o_yp", [NST * TP, D])
    o_ys = dout("o_ys", [NSAMP * 4, D])
    o_cp = dout("o_cp", [NE * 2, 2048])
    o_cs = dout("o_cs", [NE * NSAMP * 2, 2048])
    o_kp = dout("o_kp", [NOD * 128, 512])
    o_vp = dout("o_vp", [NOD * 128, 512])
    o_ks = dout("o_ks", [NOD * NSAMP * 128, 512])
    o_vs = dout("o_vs", [NOD * NSAMP * 128, 512])
    o_cv = dout("o_cv", [NE * NSAMP * 4, 2048])
    bm_dram = nc.dram_tensor("bm_dram", [8 * 128, 2048], F32, kind="Internal").ap()

    def sb(name, shape, dt):
        return nc.alloc_sbuf_tensor(name, list(shape), dt)

    xT = sb("xT", [128, KC, TT], F32)
    yT = sb("yT", [128, KC, TT], F32)
    hT = sb("hT", [128, KC, TT], BF16)
    mixT = sb("mixT", [128, KC, TT], BF16)
    slots = sb("slots", [128, NSLOT, 8, 512], BF16)
    SBYTES = 57600
    SR = sb("SR", [128, SBYTES // 4], F32)
    ident = sb("ident", [128, 128], F32)
    identb = sb("identb", [128, 128], BF16)
    ones32 = sb("ones32", [128, 128], F32)
    onesb = sb("onesb", [1, 128], BF16)
    ones16 = sb("ones16", [128, 128], BF16)
    fmask = sb("fmask", [128, 256], F32)
    gT = sb("gT", [128, KC, 16], F32)
    cwT = sb("cwT", [128, 16, 8], F32)
    scT = sb("scT", [128, 16, 16], F32)
    WtS = sb("WtS", [128, NE * 8, 128], BF16)
    bhi = sb("bhi", [1, NE * 8 * 128], BF16)
    blo = sb("blo", [1, NE * 8 * 128], BF16)
    sinkbc = sb("sinkbc", [128, NOD * 64], F32)
    zhalo = sb("zhalo", [128, NE, 16, 2], F32)
    kTh = sb("kTh", [128, NOD, 8, 128], BF16)
    vdh = sb("vdh", [128, NOD, 8, 128], BF16)
    rstd = sb("rstd", [128, TT], F32)
    small = sb("small", [128, 64], F32)
    P.scr = sb("scr", [1, 8], F32)
    ps = nc.alloc_psum_tensor("ps", [128, 8, 512], F32)
    psb3 = ps.bitcast(BF16)

    class Arena:
        def __init__(self, base, nwords):
            self.base, self.n, self.off = base, nwords, 0

        def reset(self):
            self.off = 0

        def take(self, shape, dt):
            per = int(np.prod(shape[1:]))
            words = (per * (2 if dt == BF16 else 4) + 3) // 4
            words = (words + 7) // 8 * 8
            assert self.off + words <= self.n, ("arena overflow", shape, self.off, words, self.n)
            v = self.base[:, self.off:self.off + words]
            self.off += words
            if dt == BF16:
                v = v.bitcast(BF16)[:, 0:per]
            else:
                v = v[:, 0:per]
            if len(shape) == 2:
                return v[0:shape[0], :]
            names = " ".join(f"a{i}" for i in range(len(shape) - 1))
            kw = {f"a{i}": shape[i + 1] for i in range(len(shape) - 1)}
            return v.rearrange(f"p ({names}) -> p {names}", **kw)[0:shape[0]]

    arS = Arena(SR, SBYTES // 4)
    arY = Arena(yT.rearrange("p a b -> p (a b)") if hasattr(yT, "rearrange") else yT[:].rearrange("p a b -> p (a b)"), KC * TT)

    bank_rr = [0]

    def banks(n):
        b0 = bank_rr[0]
        if b0 + n > 8:
            b0 = 0
        bank_rr[0] = (b0 + n) % 8
        return list(range(b0, b0 + n))

    bbank_rr = [0]

    def bbanks(n):
        b0 = bbank_rr[0]
        if b0 + n > 2:
            b0 = 0
        bbank_rr[0] = (b0 + n) % 2
        return list(range(b0, b0 + n))

    def bbkeys(bl):
        return tuple(f"B{b}" for b in bl)

    def bkeys(bl):
        return tuple(f"P{b}" for b in bl)

    jobs = []
    state = {"issued": 0, "next": 0}
    slot_sems = [[P.new_sem(None), 0] for _ in range(NSLOT)]

    wbf = [None, 0]

    def issue_upto(j):
        while state["issued"] <= min(j, len(jobs) - 1):
            jj = state["issued"]
            s = jj % NSLOT
            w, r0, nk, c0, ncol = jobs[jj]
            JPS = wbf[1]
            jl = jj % JPS
            cache = wbf[0][jl // 200][(jl % 200) * 128:(jl % 200 + 1) * 128, 0:nk * ncol].rearrange("p (k n) -> p k n", n=ncol)
            if jj < JPS or not USE_WCACHE:
                src = w[r0:r0 + nk * 128, c0:c0 + ncol].rearrange("(k p) n -> p k n", p=128)
                P.dma("pool", slots[:, s, 0:nk, 0:ncol], src, reads=(), writes=(f"slot{s}",), sem=slot_sems[s])
                if USE_WCACHE and NST > 1:
                    P.dma("sp", cache, slots[:, s, 0:nk, 0:ncol], reads=(f"slot{s}",), writes=(f"wb{jl}",), noauto=True)
            else:
                P.dma("pool", slots[:, s, 0:nk, 0:ncol], cache, reads=(f"wb{jl}",), writes=(f"slot{s}",), sem=slot_sems[s])
            state["issued"] += 1

    def next_job(w, r0, nk, c0, ncol):
        if P.collect:
            jobs.append((w, r0, nk, c0, ncol))
            return None
        j = state["next"]
        state["next"] += 1
        assert jobs[j][1:] == (r0, nk, c0, ncol), "job order mismatch"
        issue_upto(j + NSLOT - 1)
        return j % NSLOT

    def linear_fm(w, wr0, col0, ncols, inT, kcn, T, in_keys, evac):
        noc = ncols // 128
        bl = banks(noc)
        nj = (kcn + 7) // 8
        for jq in range(nj):
            nk = min(8, kcn - jq * 8)
            s = next_job(w, wr0 + jq * 1024, nk, col0, ncols)
            if P.collect:
                continue

            def fn(e, s=s, jq=jq, nk=nk):
                ins = None
                for oc in range(noc):
                    for k in range(nk):
                        kk = jq * 8 + k
                        ins = e.matmul(ps[:, bl[oc], 0:T], lhsT=slots[:, s, k, oc * 128:(oc + 1) * 128],
                                       rhs=inT[:, kk, 0:T], start=(kk == 0), stop=(kk == kcn - 1))
                return ins
            P.op("pe", fn, reads=(f"slot{s}",) + tuple(in_keys), writes=bkeys(bl))
        if not P.collect:
            for oc in range(noc):
                evac(oc, ps[:, bl[oc], 0:T], f"P{bl[oc]}")

    def linear_tm(w, wr0, col0, ncols, inT, kcn, tbs, in_keys, evac):
        bl = banks(len(tbs))
        nj = (kcn + 7) // 8
        for jq in range(nj):
            nk = min(8, kcn - jq * 8)
            s = next_job(w, wr0 + jq * 1024, nk, col0, ncols)
            if P.collect:
                continue

            def fn(e, s=s, jq=jq, nk=nk):
                ins = None
                for bi, (c0, n) in enumerate(tbs):
                    for k in range(nk):
                        kk = jq * 8 + k
                        ins = e.matmul(ps[0:n, bl[bi], 0:ncols], lhsT=inT[:, kk, c0:c0 + n],
                                       rhs=slots[:, s, k, 0:ncols], start=(kk == 0), stop=(kk == kcn - 1))
                return ins
            P.op("pe", fn, reads=(f"slot{s}",) + tuple(in_keys), writes=bkeys(bl))
        if not P.collect:
            for bi, (c0, n) in enumerate(tbs):
                evac(bi, ps[0:n, bl[bi], 0:ncols], f"P{bl[bi]}", n)

    def V(fn, r=(), w=()):
        return P.op("dve", fn, r, w)

    def A(fn, r=(), w=()):
        return P.op("act", fn, r, w)

    def PE(fn, r=(), w=()):
        return P.op("pe", fn, r, w)

    def gelu_evac(pin, pkey, out, outkey, tmp, tkey, extra_out=None):
        t0, t1 = tmp
        A(lambda e: e.activation(out=t0, in_=pin, func=AF.Square), (pkey, "S"), (tkey + "0",))
        V(lambda e: e.tensor_scalar(out=t0, in0=t0, scalar1=0.044715, scalar2=1.0, op0=ALU.mult, op1=ALU.add),
          (), (tkey + "0",))
        V(lambda e: e.tensor_tensor(out=t0, in0=t0, in1=pin, op=ALU.mult), (pkey,), (tkey + "0",))
        A(lambda e: e.activation(out=t1, in_=t0, func=AF.Sigmoid, scale=1.5957691216057308), (tkey + "0",), (tkey + "1",))
        if extra_out is not None:
            V(lambda e: e.tensor_tensor(out=extra_out[0], in0=t1, in1=pin, op=ALU.mult), (tkey + "1", pkey), (extra_out[1],))
        V(lambda e: e.tensor_tensor(out=out, in0=t1, in1=pin, op=ALU.mult), (tkey + "1", pkey), (outkey,))

    def rstd_from_bank(bl, T):
        V(lambda e: e.tensor_scalar(out=rstd[:, 0:T], in0=ps[:, bl[0], 0:T], scalar1=1.0 / D, scalar2=EPS,
                                    op0=ALU.mult, op1=ALU.add), bkeys(bl), ("rstd",))
        A(lambda e: e.activation(out=rstd[:, 0:T], in_=rstd[:, 0:T], func=AF.Sqrt), ("rstd",), ("rstd",))
        V(lambda e: e.reciprocal(out=rstd[:, 0:T], in_=rstd[:, 0:T]), ("rstd",), ("rstd",))

    def rms_stats(src, srckey, sq, sqkey, T):
        bl = banks(1)
        for j in range(4):
            A(lambda e, j=j: e.activation(out=sq[:, j * 8:(j + 1) * 8, 0:T], in_=src[:, j * 8:(j + 1) * 8, 0:T], func=AF.Square),
              (srckey,), (sqkey, f"sqc{j}"))

            def fn(e, j=j):
                ins = None
                for k in range(j * 8, (j + 1) * 8):
                    ins = e.matmul(ps[:, bl[0], 0:T], lhsT=ones16[:, :], rhs=sq[:, k, 0:T], start=(k == 0), stop=(k == KC - 1))
                return ins
            PE(fn, (f"sqc{j}", "ones16"), bkeys(bl))
        rstd_from_bank(bl, T)

    nstate = {"have": False}

    def pre_norm(gi, T):
        if not nstate["have"]:
            arY.reset()
            sq = arY.take([128, KC, TT], BF16)
            rms_stats(xT, "xT", sq, "Y", T)
        for k in range(KC):
            V(lambda e, k=k: e.scalar_tensor_tensor(out=hT[:, k, 0:T], in0=xT[:, k, 0:T], scalar=gT[:, k, gi:gi + 1],
                                                    in1=rstd[:, 0:T], op0=ALU.mult, op1=ALU.mult),
              ("xT", "rstd", "gT"), ("hT",))
        if not nstate["have"]:
            P.barrier("Y")
        nstate["have"] = False

    def post_norm(gi, T, fuse_next):
        P.barrier("S")
        arS.reset()
        sq = arS.take([128, KC, TT], BF16)
        sq2 = arS.take([128, KC, TT], BF16)
        tmp = arS.take([128, 2, TT], F32)
        rms_stats(yT, "Y", sq, "S", T)
        bl2 = banks(1) if fuse_next else None
        for k in range(KC):
            V(lambda e, k=k: e.scalar_tensor_tensor(out=tmp[:, k % 2, 0:T], in0=yT[:, k, 0:T], scalar=gT[:, k, gi:gi + 1],
                                                    in1=rstd[:, 0:T], op0=ALU.mult, op1=ALU.mult),
              ("Y", "rstd", "gT"), (f"pn{k % 2}",))
            V(lambda e, k=k: e.tensor_tensor(out=xT[:, k, 0:T], in0=xT[:, k, 0:T], in1=tmp[:, k % 2, 0:T], op=ALU.add),
              (f"pn{k % 2}",), ("xT", f"xTk{k}"))
            if fuse_next:
                A(lambda e, k=k: e.activation(out=sq2[:, k, 0:T], in_=xT[:, k, 0:T], func=AF.Square), (f"xTk{k}",), (f"sq2_{k}",))
                PE(lambda e, k=k: e.matmul(ps[:, bl2[0], 0:T], lhsT=ones16[:, :], rhs=sq2[:, k, 0:T], start=(k == 0), stop=(k == KC - 1)),
                   (f"sq2_{k}", "ones16"), bkeys(bl2))
        if fuse_next:
            rstd_from_bank(bl2, T)
            nstate["have"] = True
        P.barrier("S")

    def ffn(l, T):
        arS.reset()
        actT = arS.take([128, FC, TT], BF16)
        sg = arS.take([128, 1, 4, TT], F32)
        ng = (FC + 3) // 4
        for g in range(ng):
            noc = min(4, FC - g * 4)

            def ev_g(oc, pap, pkey, g=g):
                A(lambda e: e.activation(out=sg[:, 0, oc, 0:T], in_=pap, func=AF.Silu), (pkey, "S"), (f"sg_{oc}",))
            linear_fm(wg, l * D, g * 512, noc * 128, hT, KC, T, ("hT",), ev_g)

            def ev_u(oc, pap, pkey, g=g):
                V(lambda e: e.tensor_tensor(out=actT[:, g * 4 + oc, 0:T], in0=sg[:, 0, oc, 0:T], in1=pap, op=ALU.mult),
                  (pkey, f"sg_{oc}", "S"), (f"act{g * 4 + oc}",))
            linear_fm(wu, l * D, g * 512, noc * 128, hT, KC, T, ("hT",), ev_u)
        akeys = tuple(f"act{c}" for c in range(FC))
        for og in range(8):
            def ev_d(oc, pap, pkey, og=og):
                if oc % 2 == 0:
                    A(lambda e: e.activation(out=yT[:, og * 4 + oc, 0:T], in_=pap, func=AF.Copy), (pkey, "Y"), (f"y{og * 4 + oc}",))
                else:
                    V(lambda e: e.tensor_copy(out=yT[:, og * 4 + oc, 0:T], in_=pap), (pkey, "Y"), (f"y{og * 4 + oc}",))
            linear_fm(wd, l * DFF, og * 512, 512, actT, FC, T, akeys + ("S",), ev_d)
        if not P.collect:
            V(lambda e: e.memset(P.scr[0:1, 1:2], 0.0), tuple(f"y{c}" for c in range(KC)), ("Y",))

    def out_proj(w, wr0, T):
        for og in range(8):
            def ev_o(oc, pap, pkey, og=og):
                if oc % 2 == 0:
                    A(lambda e: e.activation(out=yT[:, og * 4 + oc, 0:T], in_=pap, func=AF.Copy), (pkey, "Y"), (f"y{og * 4 + oc}",))
                else:
                    V(lambda e: e.tensor_copy(out=yT[:, og * 4 + oc, 0:T], in_=pap), (pkey, "Y"), (f"y{og * 4 + oc}",))
            linear_fm(w, wr0, og * 512, 512, mixT, KC, T, ("mixT",), ev_o)
        if not P.collect:
            V(lambda e: e.memset(P.scr[0:1, 1:2], 0.0), tuple(f"y{c}" for c in range(KC)), ("Y",))

    def even_mixer(l, st, T, si):
        i = l // 2
        has_s = si is not None
        arS.reset()
        uT = arS.take([128, 16, TT], BF16)
        vtm = arS.take([128, NBLK, 2048], BF16)
        vsb = arS.take([4, 2048], BF16)
        vs32 = arS.take([4, 2, 512], F32)
        zT = arS.take([128, 16, TT + 4], F32)
        gtmp = arS.take([128, 2, 512], F32)
        xbt = arS.take([128, 4, TT], F32)
        acc = arS.take([128, 2, TT + 2], F32)
        wr0 = i * D
        for j in range(4):
            def ev(oc, pap, pkey, j=j):
                c = j * 4 + oc
                gelu_evac(pap, pkey, uT[:, c, 0:T], f"u{c}", (gtmp[:, 0, 0:T], gtmp[:, 1, 0:T]), "gt")
            linear_fm(w_in, wr0, j * 512, 512, hT, KC, T, ("hT", "S"), ev)
        tbs = [(b * 128, 128) for b in range(NBLK)] + ([(TP, 4)] if has_s else [])
        for j in range(4):
            def ev(bi, pap, pkey, n, j=j):
                if bi < NBLK:
                    gelu_evac(pap, pkey, vtm[:, bi, j * 512:(j + 1) * 512], f"v{bi}", (gtmp[:, 0, :], gtmp[:, 1, :]), "gt")
                else:
                    gelu_evac(pap, pkey, vs32[0:4, j % 2, :], f"vs32{j % 2}", (gtmp[0:4, 0, :], gtmp[0:4, 1, :]), "gt",
                              extra_out=(vsb[0:4, j * 512:(j + 1) * 512], "vsb"))
                    r_ = (i * NSAMP + si) * 4
                    P.dma("sp", o_cv[r_:r_ + 4, j * 512:(j + 1) * 512], vs32[0:4, j % 2, :], reads=(f"vs32{j % 2}",), is_out=True)
            linear_tm(w_in, wr0, 2048 + j * 512, 512, hT, KC, tbs, ("hT", "S"), ev)
        if not P.collect:
            for c in range(16):
                hidx = i * 8 + c // 2
                bl = banks(1)

                def fn(e, c=c, hidx=hidx, bl=bl):
                    ins = None
                    for b in range(NBLK):
                        o = ps[:, bl[0], b * 128:(b + 1) * 128]
                        e.matmul(o, lhsT=vtm[:, b, c * 128:(c + 1) * 128], rhs=WtS[:, hidx, :], start=True, stop=False)
                        e.matmul(o, lhsT=onesb[0:1, :], rhs=bhi[0:1, hidx * 128:(hidx + 1) * 128], start=False, stop=False)
                        ins = e.matmul(o, lhsT=onesb[0:1, :], rhs=blo[0:1, hidx * 128:(hidx + 1) * 128], start=False, stop=True)
                    if has_s:
                        o = ps[:, bl[0], TP:TP + 4]
                        e.matmul(o, lhsT=vsb[0:4, c * 128:(c + 1) * 128], rhs=WtS[0:4, hidx, 0:4], start=True, stop=False)
                        e.matmul(o, lhsT=onesb[0:1, :], rhs=bhi[0:1, hidx * 128:hidx * 128 + 4], start=False, stop=False)
                        ins = e.matmul(o, lhsT=onesb[0:1, :], rhs=blo[0:1, hidx * 128:hidx * 128 + 4], start=False, stop=True)
                    return ins
                PE(fn, tuple(f"v{b}" for b in range(NBLK)) + ("vsb", "WtS", "bhl", "S"), bkeys(bl))
                V(lambda e, c=c, bl=bl: e.tensor_tensor(out=mixT[:, c, 0:T], in0=uT[:, c, 0:T], in1=ps[:, bl[0], 0:T], op=ALU.mult),
                  (f"u{c}", f"P{bl[0]}"), ("mixT",))
            A(lambda e: e.activation(out=zT[:, :, 0:2], in_=zhalo[:, i, :, :], func=AF.Copy), ("zhalo", "S"), ("zh",))
            if has_s:
                r0 = (i * NSAMP + si) * 2
                A(lambda e: e.activation(out=zT[:, :, TP + 2:TP + 4], in_=scT[:, :, r0:r0 + 2], func=AF.Copy), ("scT", "S"), ("zh2",))
        for j in range(4):
            def ev_x(oc, pap, pkey):
                A(lambda e: e.activation(out=xbt[:, oc, 0:T], in_=pap, func=AF.Copy), (pkey, "S"), (f"xb{oc}",))
            linear_fm(w_in, wr0, 4096 + j * 512, 512, hT, KC, T, ("hT",), ev_x)

            def ev_c(oc, pap, pkey, j=j):
                c = j * 4 + oc
                V(lambda e: e.tensor_tensor(out=zT[:, c, 2:2 + TP], in0=xbt[:, oc, 0:TP], in1=pap[:, 0:TP], op=ALU.mult),
                  (pkey, f"xb{oc}"), (f"z{c}",))
                if has_s:
                    V(lambda e: e.tensor_tensor(out=zT[:, c, TP + 4:TP + 8], in0=xbt[:, oc, TP:TP + 4], in1=pap[:, TP:TP + 4], op=ALU.mult),
                      (pkey, f"xb{oc}"), (f"zs{c}",))
            linear_fm(w_in, wr0, 6144 + j * 512, 512, hT, KC, T, ("hT",), ev_c)

            def ev_b(oc, pap, pkey, j=j):
                c = j * 4 + oc
                n = TP + 6 if has_s else TP
                a = acc[:, c % 2, 0:n]
                zk = (f"z{c}", f"zs{c}", "zh", "zh2")
                V(lambda e: e.tensor_scalar(out=a, in0=zT[:, c, 2:2 + n], scalar1=cwT[:, c, i * 3 + 2:i * 3 + 3], scalar2=None, op0=ALU.mult),
                  zk + ("cwT",), (f"acc{c % 2}",))
                V(lambda e: e.scalar_tensor_tensor(out=a, in0=zT[:, c, 1:1 + n], scalar=cwT[:, c, i * 3 + 1:i * 3 + 2], in1=a,
                                                   op0=ALU.mult, op1=ALU.add), zk, (f"acc{c % 2}",))
                V(lambda e: e.scalar_tensor_tensor(out=a, in0=zT[:, c, 0:n], scalar=cwT[:, c, i * 3:i * 3 + 1], in1=a,
                                                   op0=ALU.mult, op1=ALU.add), zk, (f"acc{c % 2}",))
                V(lambda e: e.tensor_tensor(out=mixT[:, 16 + c, 0:TP], in0=acc[:, c % 2, 0:TP], in1=pap[:, 0:TP], op=ALU.mult),
                  (pkey, f"acc{c % 2}"), ("mixT",))
                if has_s:
                    V(lambda e: e.tensor_tensor(out=mixT[:, 16 + c, TP:TP + 4], in0=acc[:, c % 2, TP + 2:TP + 6], in1=pap[:, TP:TP + 4], op=ALU.mult),
                      (pkey, f"acc{c % 2}"), ("mixT",))
            linear_fm(w_in, wr0, 8192 + j * 512, 512, hT, KC, T, ("hT",), ev_b)
        if not P.collect:
            allz = tuple(f"z{c}" for c in range(16)) + tuple(f"zs{c}" for c in range(16))
            V(lambda e: e.tensor_copy(out=zhalo[:, i, :, :], in_=zT[:, :, TP:TP + 2]), allz + ("zh",), ("zhalo",))

            def z_out(c0, dst):
                bl4 = banks(4)

                def fn2(e):
                    ins = None
                    for c in range(16):
                        ins = e.transpose(out=ps[0:2, bl4[c // 4], (c % 4) * 128:(c % 4 + 1) * 128], in_=zT[:, c, c0:c0 + 2], identity=ident[:, :])
                    return ins
                PE(fn2, allz + ("ident", "S"), bkeys(bl4))
                for q in range(4):
                    V(lambda e, q=q: e.tensor_copy(out=gtmp[0:2, q % 2, :], in_=ps[0:2, bl4[q], 0:512]), (f"P{bl4[q]}", "S"), (f"gt{q % 2}",))
                    P.dma("sp", dst[:, q * 512:(q + 1) * 512], gtmp[0:2, q % 2, :], reads=(f"gt{q % 2}",), is_out=True)
            if st == NST - 1:
                z_out(TP, o_cp[i * 2:i * 2 + 2, :])
            if has_s:
                z_out(TP + 6, o_cs[(i * NSAMP + si) * 2:(i * NSAMP + si) * 2 + 2, :])
        out_proj(w_out, wr0, T)

    def odd_mixer(l, st, T, si):
        i = l // 2
        has_s = si is not None
        arS.reset()
        qg = arS.take([128, 2, 2, 4, TT], BF16)
        kTb = arS.take([128, NBLK, 8, 128], BF16)
        vdb = arS.take([128, NBLK, 8, 128], BF16)
        kTc = arS.take([128, 8, 128], BF16)
        vdc = arS.take([128, 8, 128], BF16)
        kTn = arS.take([128, 8, 4], BF16)
        vdn = arS.take([4, 8, 128], BF16)
        kdup = arS.take([128, 8, 128], BF16)
        ktm = arS.take([128, 512], F32)
        vtm32 = arS.take([128, 512], F32)
        bm = arS.take([128, 8, 256], F32)
        s_sb = arS.take([128, 8, 256], F32)
        pexp2 = arS.take([128, 1, 8, 256], BF16)
        pT = arS.take([128, 8, 2, 128], BF16)
        wr0 = i * D

        KBF = _os.environ.get("DEV_KB", "")

        def k_block(src, skey, n, kT_dst, dkey, out_dma=None, from_psum=True):
            if KBF == "10":
                return
            if from_psum:
                A(lambda e: e.activation(out=ktm[0:n, :], in_=src, func=AF.Copy), (skey, "S"), ("ktm",))
            if KBF == "11":
                return
            sv = src.rearrange("p (g x) -> p g x", g=8)
            A(lambda e: e.activation(out=kdup[0:n, :, 0:64], in_=sv, func=AF.Copy), (skey, "S"), ("kdupa",))
            A(lambda e: e.activation(out=kdup[0:n, :, 64:128], in_=sv, func=AF.Copy), (skey,), ("kdupb",))
            if KBF == "12":
                return
            if out_dma is not None:
                P.dma("sp", out_dma, ktm[0:n, :], reads=("ktm",), is_out=True)
            if _os.environ.get("DEV_KB") == "1":
                return
            bl = banks(1)
            pv = psb3[:, bl[0], :].rearrange("p (g x) -> p g x", g=8)

            def fn(e):
                ins = None
                for g in range(8):
                    ins = e.transpose(out=pv[:, g, 0:n], in_=kdup[0:n, g, :], identity=identb[0:n, 0:n])
                return ins
            PE(fn, ("kdupa", "kdupb", "identb", "S"), bkeys(bl))
            if _os.environ.get("DEV_KB") == "2":
                return
            for hb in range(2):
                A(lambda e, hb=hb: e.activation(out=kT_dst[:, hb * 4:hb * 4 + 4, 0:n], in_=pv[:, hb * 4:hb * 4 + 4, 0:n], func=AF.Copy), bkeys(bl) + ("S",), (dkey,))

        def v_block(src, skey, n, vd_dst, dkey, out_dma=None, from_psum=True):
            if KBF == "10":
                return
            if from_psum:
                V(lambda e: e.tensor_copy(out=vtm32[0:n, :], in_=src), (skey, "S"), ("vtm",))
            if KBF == "11":
                return
            sv = src.rearrange("p (g x) -> p g x", g=8)
            V(lambda e: e.tensor_copy(out=vd_dst[0:n, :, 0:64], in_=sv), (skey, "S"), (dkey + "a",))
            V(lambda e: e.tensor_copy(out=vd_dst[0:n, :, 64:128], in_=sv), (skey,), (dkey + "b",))
            if KBF == "12":
                return
            if out_dma is not None:
                P.dma("sp", out_dma, vtm32[0:n, :], reads=("vtm",), is_out=True)

        tbs = [(b * 128, 128) for b in range(NBLK)] + ([(TP, 4)] if has_s else [])
        last = (st == NST - 1)
        if int(_os.environ.get("DEV_ODD", "99")) == 0:
            out_proj(w_o, wr0, T)
            return

        def ev_k(bi, pap, pkey, n):
            if bi < NBLK:
                od = o_kp[i * 128:(i + 1) * 128, :] if (last and bi == NBLK - 1) else None
                k_block(pap, pkey, n, kTb[:, bi], f"kT{bi}", od)
            else:
                r = (i * NSAMP + si) * 128
                k_block(pap, pkey, n, kTn, "kTn", o_ks[r + 124:r + 128, :])
        linear_tm(w_qkv, wr0, 4096, 512, hT, KC, tbs, ("hT", "S"), ev_k)

        def ev_v(bi, pap, pkey, n):
            if bi < NBLK:
                od = o_vp[i * 128:(i + 1) * 128, :] if (last and bi == NBLK - 1) else None
                v_block(pap, pkey, n, vdb[:, bi], f"vd{bi}", od)
            else:
                r = (i * NSAMP + si) * 128
                v_block(pap, pkey, n, vdn, "vdn", o_vs[r + 124:r + 128, :])
        linear_tm(w_qkv, wr0, 4608, 512, hT, KC, tbs, ("hT", "S"), ev_v)
        ODD = int(_os.environ.get("DEV_ODD", "99"))
        if ODD == 1:
            out_proj(w_o, wr0, T)
            return
        if has_s and not P.collect:
            r = (i * NSAMP + si) * 128
            P.dma("sp", ktm[:, :], cwk[r:r + 128, :], reads=("S",), writes=("ktm",))
            P.dma("sp", o_ks[r:r + 124, :], ktm[4:128, :], reads=("ktm",), is_out=True)
            k_block(ktm[:, :], "ktm", 128, kTc, "kTc", None, from_psum=False)
            P.dma("sp", vtm32[:, :], cwv[r:r + 128, :], reads=("S",), writes=("vtm",))
            P.dma("sp", o_vs[r:r + 124, :], vtm32[4:128, :], reads=("vtm",), is_out=True)
            v_block(vtm32[:, :], "vtm", 128, vdc, "vdc", None, from_psum=False)
        if ODD == 2:
            out_proj(w_o, wr0, T)
            return
        if not P.collect:
            V(lambda e: e.memset(qg[64:128, :, 0], 0.0), ("S",), ("qz",))
            V(lambda e: e.memset(qg[0:64, :, 1], 0.0), ("S",), ("qz",))
        qblocks = []
        for b in range(NBLK):
            if b == 0:
                prev = (kTh[:, i], vdh[:, i], 128, ("kTh",), ("vdh",))
            else:
                prev = (kTb[:, b - 1], vdb[:, b - 1], 128, (f"kT{b - 1}",), (f"vd{b - 1}a", f"vd{b - 1}b"))
            own = (kTb[:, b], vdb[:, b], 128, (f"kT{b}",), (f"vd{b}a", f"vd{b}b"))
            qblocks.append((b * 128, 128, [prev, own], (st == 0 and b == 0)))
        if has_s:
            qblocks.append((TP, 4, [(kTc, vdc, 128, ("kTc",), ("vdca", "vdcb")), (kTn, vdn, 4, ("kTn",), ("vdna", "vdnb"))], False))
        def stage1(u, g, q0, QB, segs, first):
            par = 0
            NK = 128 + segs[1][2]
            bs = banks(4)
            psv = ps[:, bs[0]:bs[0] + 4, :].rearrange("p b (h k) -> p (b h) k", h=2)

            def fqk(e):
                ins = None
                for ii in range(8):
                    off = 0
                    for (kt, vd, n, kk, vk) in segs:
                        ins = e.matmul(psv[0:QB, ii, off:off + n], lhsT=qg[:, g % 2, ii % 2, ii // 2, q0:q0 + QB], rhs=kt[:, g, 0:n],
                                       start=True, stop=True)
                        off += n
                return ins
            PE(fqk, tuple(f"ql{g % 2}_{x}" for x in range(4)) + tuple(f"qh{g % 2}_{x}" for x in range(4)) + ("qz",) + segs[0][3] + segs[1][3] + ("S",), bkeys(bs))
            V(lambda e: e.tensor_tensor(out=s_sb[0:QB, :, 0:NK], in0=psv[0:QB, :, 0:NK], in1=bm[0:QB, :, 0:NK], op=ALU.add),
              bkeys(bs) + ("bm", "S"), ("s_sb",))
            if first:
                for ii in range(8):
                    V(lambda e, ii=ii: e.tensor_tensor(out=s_sb[:, ii, :], in0=s_sb[:, ii, :], in1=fmask[:, :], op=ALU.add), ("fmask",), ("s_sb",))
            mx, negm, sm, rs, es, den = (small[0:QB, 0:8], small[0:QB, 8:16], small[0:QB, 16:24], small[0:QB, 24:32],
                                         small[0:QB, 32:40], small[0:QB, 40:48])
            skc = sinkbc[0:QB, i * 64 + g * 8:i * 64 + g * 8 + 8]
            rsk = tuple(f"rs{ii}" for ii in range(8))
            V(lambda e: e.tensor_reduce(out=mx, in_=s_sb[0:QB, :, 0:NK], axis=AX.X, op=ALU.max), ("s_sb",), ("mx",))
            V(lambda e: e.tensor_tensor(out=mx, in0=mx, in1=skc, op=ALU.max), ("sinkbc",), ("mx",))
            V(lambda e: e.tensor_scalar(out=negm, in0=mx, scalar1=-1.0, scalar2=None, op0=ALU.mult), ("mx",), ("negm",))
            V(lambda e: e.tensor_tensor(out=sm, in0=skc, in1=mx, op=ALU.subtract), ("mx", "sinkbc"), ("sm",))
            V(lambda e: e.memset(rs, 0.0), (), rsk)

        def stage1b(u, g, q0, QB, segs, first):
            par = 0
            NK = 128 + segs[1][2]
            mx, negm, sm, rs, es, den = (small[0:QB, 0:8], small[0:QB, 8:16], small[0:QB, 16:24], small[0:QB, 24:32],
                                         small[0:QB, 32:40], small[0:QB, 40:48])
            rsk = tuple(f"rs{ii}" for ii in range(8))
            for ii in range(8):
                A(lambda e, ii=ii: e.activation(out=pexp2[0:QB, par, ii, 0:NK], in_=s_sb[0:QB, ii, 0:NK], func=AF.Exp,
                                                bias=negm[:, ii:ii + 1], scale=1.0, accum_out=rs[:, ii:ii + 1]),
                  ("s_sb", "negm"), (f"pe{par}_{ii}", f"rs{ii}"))
            A(lambda e: e.activation(out=es, in_=sm, func=AF.Exp), ("sm",), ("es",))
            V(lambda e: e.tensor_tensor(out=den, in0=rs, in1=es, op=ALU.add), rsk + ("es",), ("den",))
            V(lambda e: e.reciprocal(out=den, in_=den), (), ("den",))
            for ii in range(8):
                if ii < 6:
                    V(lambda e, ii=ii: e.tensor_scalar(out=pexp2[0:QB, par, ii, 0:NK], in0=pexp2[0:QB, par, ii, 0:NK], scalar1=den[:, ii:ii + 1],
                                                       scalar2=None, op0=ALU.mult), ("den",), (f"pe{par}_{ii}",))
                else:
                    A(lambda e, ii=ii: e.activation(out=pexp2[0:QB, par, ii, 0:NK], in_=pexp2[0:QB, par, ii, 0:NK], func=AF.Copy,
                                                    scale=den[:, ii:ii + 1]), ("den",), (f"pe{par}_{ii}",))

        def stage2(u, g, q0, QB, segs):
            par = 0
            n0, n1 = segs[0][2], segs[1][2]
            bt = banks(2)
            ptv = psb3[:, bt[0]:bt[0] + 2, :].rearrange("p b (h s q) -> p (b h) s q", h=4, s=2)

            def ftr(e):
                ins = None
                for ii in range(8):
                    off = 0
                    for sx, (kt, vd, n, kk, vk) in enumerate(segs):
                        ins = e.transpose(out=ptv[0:n, ii, sx, 0:QB], in_=pexp2[0:QB, par, ii, off:off + n], identity=identb[0:QB, 0:QB])
                        off += n
                return ins
            PE(ftr, tuple(f"pe{par}_{ii}" for ii in range(8)) + ("identb", "S"), bkeys(bt))
            for hf, opf in ((0, A), (1, V)):
                for sx, nn in ((0, n0), (1, n1)):
                    if opf is A:
                        A(lambda e, hf=hf, sx=sx, nn=nn: e.activation(out=pT[0:nn, hf * 4:hf * 4 + 4, sx, 0:QB], in_=ptv[0:nn, hf * 4:hf * 4 + 4, sx, 0:QB], func=AF.Copy),
                          (f"P{bt[hf]}", "S"), (f"pT{sx}{hf}",))
                    else:
                        V(lambda e, hf=hf, sx=sx, nn=nn: e.tensor_copy(out=pT[0:nn, hf * 4:hf * 4 + 4, sx, 0:QB], in_=ptv[0:nn, hf * 4:hf * 4 + 4, sx, 0:QB]),
                          (f"P{bt[hf]}", "S"), (f"pT{sx}{hf}",))
            bo = banks(2)
            pov = ps[:, bo[0]:bo[0] + 2, :].rearrange("p b (h q) -> p (b h) q", h=4)

            def fpv(e):
                ins = None
                for ii in range(8):
                    for sx, (kt, vd, n, kk, vk) in enumerate(segs):
                        ins = e.matmul(pov[:, ii, 0:QB], lhsT=vd[0:n, g, :], rhs=pT[0:n, ii, sx, 0:QB], start=(sx == 0), stop=(sx == 1))
                return ins
            PE(fpv, ("pT00", "pT01", "pT10", "pT11", "S") + segs[0][4] + segs[1][4], bkeys(bo))
            pe4 = pov.rearrange("p (a two) q -> p a two q", two=2)
            A(lambda e: e.activation(out=mixT[0:64, g * 4:g * 4 + 2, q0:q0 + QB], in_=pe4[0:64, 0:2, 0, 0:QB], func=AF.Copy), (f"P{bo[0]}",), ("mixT",))
            A(lambda e: e.activation(out=mixT[64:128, g * 4:g * 4 + 2, q0:q0 + QB], in_=pe4[64:128, 0:2, 1, 0:QB], func=AF.Copy), (f"P{bo[0]}",), ("mixT",))
            V(lambda e: e.tensor_copy(out=mixT[0:64, g * 4 + 2:g * 4 + 4, q0:q0 + QB], in_=pe4[0:64, 2:4, 0, 0:QB]), (f"P{bo[1]}",), ("mixT",))
            V(lambda e: e.tensor_copy(out=mixT[64:128, g * 4 + 2:g * 4 + 4, q0:q0 + QB], in_=pe4[64:128, 2:4, 1, 0:QB]), (f"P{bo[1]}",), ("mixT",))

        pend = None
        ucnt = 0
        for g in range(8):
            def ev_q(oc, pap, pkey, g=g):
                if oc % 2 == 0:
                    A(lambda e: e.activation(out=qg[0:64, g % 2, 0, oc, 0:T], in_=pap[0:64, :], func=AF.Copy, scale=0.125), (pkey, "S", "qz"), (f"ql{g % 2}_{oc}",))
                    A(lambda e: e.activation(out=qg[64:128, g % 2, 1, oc, 0:T], in_=pap[64:128, :], func=AF.Copy, scale=0.125), (pkey, "S", "qz"), (f"qh{g % 2}_{oc}",))
                else:
                    V(lambda e: e.tensor_scalar(out=qg[0:64, g % 2, 0, oc, 0:T], in0=pap[0:64, :], scalar1=0.125, scalar2=None, op0=ALU.mult),
                      (pkey, "S", "qz"), (f"ql{g % 2}_{oc}",))
                    V(lambda e: e.tensor_scalar(out=qg[64:128, g % 2, 1, oc, 0:T], in0=pap[64:128, :], scalar1=0.125, scalar2=None, op0=ALU.mult),
                      (pkey, "S", "qz"), (f"qh{g % 2}_{oc}",))
            linear_fm(w_qkv, wr0, g * 512, 512, hT, KC, T, ("hT",), ev_q)
            if P.collect:
                continue
            P.dma("sp", bm.rearrange("p a b -> p (a b)"), bm_dram[g * 128:(g + 1) * 128, :], reads=("S",), writes=("bm",))
            for (q0, QB, segs, first) in qblocks:
                stage1(ucnt, g, q0, QB, segs, first)
                if pend is not None:
                    stage2(*pend)
                stage1b(ucnt, g, q0, QB, segs, first)
                pend = (ucnt, g, q0, QB, segs)
                ucnt += 1
        if pend is not None:
            stage2(*pend)
        if not P.collect:
            lb = NBLK - 1
            V(lambda e: e.tensor_copy(out=kTh[:, i], in_=kTb[:, lb]), (f"kT{lb}", "S"), ("kTh",))
            A(lambda e: e.activation(out=vdh[:, i], in_=vdb[:, lb], func=AF.Copy), (f"vd{lb}a", f"vd{lb}b", "S"), ("vdh",))
        out_proj(w_o, wr0, T)

    def setup():
        if P.collect:
            return
        arS.reset()
        cS = arS.take([128, 1280], F32)
        P.dma("sp", cS, cst[:, :], writes=("cS",))
        V(lambda e: e.tensor_copy(out=ident[:, :], in_=cS[:, 0:128]), ("cS",), ("ident",))
        V(lambda e: e.tensor_copy(out=identb[:, :], in_=cS[:, 0:128]), ("cS",), ("identb",))
        V(lambda e: e.tensor_copy(out=fmask[:, :], in_=cS[:, 895:1151]), ("cS",), ("fmask",))
        V(lambda e: e.memset(ones32[:, :], 1.0), (), ("ones32",))
        V(lambda e: e.memset(ones16[:, :], 1.0), (), ("ones16",))
        V(lambda e: e.memset(onesb[:, :], 1.0), (), ("bhl",))
        V(lambda e: e.memset(zhalo[:, :, :, :], 0.0), (), ("zhalo",))
        V(lambda e: e.memset(kTh[:, :, :, :], 0.0), (), ("kTh",))
        V(lambda e: e.memset(vdh[:, :, :, :], 0.0), (), ("vdh",))
        tok = arS.take([16, 4096], F32)

        def tr_rows(src_dram, nrows, ncols, dst_fn, name):
            P.dma("sp", tok[0:nrows, 0:ncols], src_dram, reads=(), writes=("tok",))
            nch = ncols // 128
            for c0 in range(0, nch, 16):
                bl = banks(1)

                def fn(e, c0=c0, bl=bl):
                    ins = None
                    for c in range(c0, min(nch, c0 + 16)):
                        ins = e.transpose(out=ps[:, bl[0], (c - c0) * 16:(c - c0) * 16 + nrows], in_=tok[0:nrows, c * 128:(c + 1) * 128],
                                          identity=ident[0:nrows, 0:nrows])
                    return ins
                PE(fn, ("tok", "ident"), bkeys(bl))
                for c in range(c0, min(nch, c0 + 16)):
                    V(lambda e, c=c, c0=c0, bl=bl: e.tensor_copy(out=dst_fn(c), in_=ps[:, bl[0], (c - c0) * 16:(c - c0) * 16 + nrows]),
                      bkeys(bl), (name,))
        tr_rows(gall[:, :], 16, 4096, lambda c: gT[:, c, :], "gT")
        tr_rows(convw[:, :], NE * 3, 2048, lambda c: cwT[:, c, 0:NE * 3], "cwT")
        tr_rows(scv[:, :], NE * NSAMP * 2, 2048, lambda c: scT[:, c, 0:NE * NSAMP * 2], "scT")
        wtmp = arS.take([128, NE * 8, 128], F32)
        P.dma("sp", wtmp, sguw.rearrange("(a i) j -> i a j", i=128), writes=("wtmp",))
        for a in range(NE * 8):
            bl = banks(1)
            PE(lambda e, a=a, bl=bl: e.transpose(out=ps[:, bl[0], 0:128], in_=wtmp[:, a, :], identity=ident[:, :]), ("wtmp", "ident"), bkeys(bl))
            V(lambda e, a=a, bl=bl: e.tensor_tensor(out=WtS[:, a, :], in0=ps[:, bl[0], 0:128], in1=cS[:, 128:256], op=ALU.mult), bkeys(bl) + ("cS",), ("WtS",))
        b32 = arS.take([1, NE * 8 * 128], F32)
        b32b = arS.take([1, NE * 8 * 128], F32)
        P.dma("sp", b32, sgub[:, :], writes=("b32",))
        V(lambda e: e.tensor_copy(out=bhi[:, :], in_=b32), ("b32",), ("bhl",))
        V(lambda e: e.tensor_copy(out=b32b, in_=bhi[:, :]), ("bhl",), ("b32b",))
        V(lambda e: e.tensor_tensor(out=b32b, in0=b32, in1=b32b, op=ALU.subtract), ("b32",), ("b32b",))
        V(lambda e: e.tensor_copy(out=blo[:, :], in_=b32b), ("b32b",), ("bhl",))
        if NO > 0:
            srow = arS.take([1, NOD * 64], F32)
            P.dma("sp", srow, sinks[:, :], writes=("srow",))
            bl = banks(1)
            PE(lambda e, bl=bl: e.matmul(ps[:, bl[0], 0:NOD * 64], lhsT=ones32[0:1, :], rhs=srow, start=True, stop=True), ("srow", "ones32"), bkeys(bl))
            V(lambda e, bl=bl: e.tensor_copy(out=sinkbc[:, :], in_=ps[:, bl[0], 0:NOD * 64]), bkeys(bl), ("sinkbc",))
            rb = arS.take([32, 64], F32)
            fdh = arS.take([128, 64], F32)
            P.dma("sp", rb, relb[:, :], writes=("rb",))
            bl = banks(1)
            PE(lambda e, bl=bl: e.matmul(ps[:, bl[0], 0:64], lhsT=cS[0:32, 1151:1279], rhs=rb, start=True, stop=True), ("rb", "cS"), bkeys(bl))
            V(lambda e, bl=bl: e.tensor_copy(out=fdh, in_=ps[:, bl[0], 0:64]), bkeys(bl), ("fdh",))
            bmg = arS.take([128, 1, 8, 256], F32)
            for g in range(8):
                bl = banks(4)
                pv = ps[:, bl[0]:bl[0] + 4, :].rearrange("p b (k h) -> p (b k) h", h=8)

                def fn(e, g=g, pv=pv):
                    ins = None
                    for k in range(256):
                        ins = e.matmul(pv[:, k, :], lhsT=cS[:, 256 + 255 - k:256 + 255 - k + 128], rhs=fdh[:, g * 8:(g + 1) * 8], start=True, stop=True)
                    return ins
                PE(fn, ("cS", "fdh"), bkeys(bl))
                for ii in range(8):
                    V(lambda e, g=g, ii=ii, pv=pv: e.tensor_tensor(out=bmg[:, 0, ii, :], in0=pv[:, :, ii], in1=cS[:, 639:895], op=ALU.add),
                      bkeys(bl) + ("cS",), ("bmg0",))
                P.dma("sp", bm_dram[g * 128:(g + 1) * 128, :], bmg[:, 0].rearrange("p a b -> p (a b)"), reads=("bmg0",), writes=("bmdram",))
        P.barrier("S")
        P.op("sp", lambda e: e.dma_start(out=P.scr[0:1, 2:3], in_=cst[0:1, 0:1]), ("bmdram",), ("S2",), inc=False) if False else None

    def load_x(st, T, si):
        if P.collect:
            return
        arS.reset()
        tok = arS.take([128, 2, 4096], F32)
        srcs = [(xp[st * TP + b * 128: st * TP + (b + 1) * 128, :], 128, b * 128) for b in range(NBLK)]
        if si is not None:
            srcs.append((xs[si * 4:si * 4 + 4, :], 4, TP))
        for n_, (src, n, c0) in enumerate(srcs):
            d = n_ % 2
            P.dma("sp", tok[0:n, d, :], src, reads=("S",), writes=(f"tok{d}",))
            for k0 in range(0, KC, 4):
                bl = banks(1)

                def fn(e, k0=k0, bl=bl, n=n, d=d):
                    ins = None
                    for k in range(k0, k0 + 4):
                        ins = e.transpose(out=ps[:, bl[0], (k - k0) * 128:(k - k0) * 128 + n], in_=tok[0:n, d, k * 128:(k + 1) * 128], identity=ident[0:n, 0:n])
                    return ins
                PE(fn, (f"tok{d}", "ident"), bkeys(bl))
                pv = ps[:, bl[0], :].rearrange("p (k t) -> p k t", k=4)
                if (k0 // 4) % 2 == 0:
                    V(lambda e, k0=k0, pv=pv, n=n, c0=c0: e.tensor_copy(out=xT[:, k0:k0 + 4, c0:c0 + n], in_=pv[:, :, 0:n]), bkeys(bl), ("xT",))
                else:
                    A(lambda e, k0=k0, pv=pv, n=n, c0=c0: e.activation(out=xT[:, k0:k0 + 4, c0:c0 + n], in_=pv[:, :, 0:n], func=AF.Copy), bkeys(bl), ("xT",))
        P.barrier("S")

    def store_x(st, T, si):
        if P.collect:
            return
        P.barrier("S")
        arS.reset()
        tok = arS.take([128, 2, 4096], F32)
        dsts = [(o_yp[st * TP + b * 128: st * TP + (b + 1) * 128, :], 128, b * 128) for b in range(NBLK)]
        if si is not None:
            dsts.append((o_ys[si * 4:si * 4 + 4, :], 4, TP))
        for n_, (dst, n, c0) in enumerate(dsts):
            d = n_ % 2
            for k0 in range(0, KC, 4):
                bl = banks(1)

                def fn(e, k0=k0, bl=bl, n=n, c0=c0):
                    ins = None
                    for k in range(k0, k0 + 4):
                        ins = e.transpose(out=ps[0:n, bl[0], (k - k0) * 128:(k - k0 + 1) * 128], in_=xT[:, k, c0:c0 + n], identity=ident[:, :])
                    return ins
                PE(fn, ("xT", "ident"), bkeys(bl))
                if (k0 // 4) % 2 == 0:
                    V(lambda e, k0=k0, bl=bl, n=n, d=d: e.tensor_copy(out=tok[0:n, d, k0 * 128:(k0 + 4) * 128], in_=ps[0:n, bl[0], :]), bkeys(bl) + ("S",), (f"tok{d}",))
                else:
                    A(lambda e, k0=k0, bl=bl, n=n, d=d: e.activation(out=tok[0:n, d, k0 * 128:(k0 + 4) * 128], in_=ps[0:n, bl[0], :], func=AF.Copy),
                      bkeys(bl) + ("S",), (f"tok{d}",))
            P.dma("sp", dst, tok[0:n, d, :], reads=(f"tok{d}",), is_out=True)
        P.barrier("S")

    STOP = int(_os.environ.get("DEV_STOP", "99"))

    def program():
        if STOP >= 0:
            setup()
        for st in range(NST):
            si = st if st < NSAMP else None
            T = TP + (4 if si is not None else 0)
            load_x(st, T, si)
            for l in range(DEPTH if STOP >= 2 else 0):
                pre_norm(0 * 4 + l, T)
                if STOP == 2:
                    continue
                if STOP == 6 and l == 0:
                    continue
                if l % 2 == 0:
                    even_mixer(l, st, T, si)
                else:
                    odd_mixer(l, st, T, si)
                if STOP == 3 or STOP == 6:
                    break
                post_norm(1 * 4 + l, T, True)
                if STOP == 4:
                    break
                pre_norm(2 * 4 + l, T)
                ffn(l, T)
                post_norm(3 * 4 + l, T, l < DEPTH - 1)
                if STOP == 5:
                    break
            store_x(st, T, si)

    P.collect = True
    program()
    P.collect = False
    bank_rr[0] = 0
    assert len(jobs) % NST == 0
    wbf[1] = len(jobs) // NST
    for j_, jb in enumerate(jobs):
        assert jb[1:] == jobs[j_ % wbf[1]][1:], "per-step job sequence differs"
    wbf[0] = [nc.dram_tensor(f"wbf{t_}", [200 * 128, 4096], BF16, kind="Internal").ap() for t_ in range((wbf[1] + 199) // 200)]
    with nc.Block() as block:
        program()
        P.emit(block)
    return nc


def host_inputs(cfg, x_prompt_core, x_sample_core, state_conv_core, ck_core, cv_core, W):
    m = dict(W)
    m["xp"] = np.ascontiguousarray(x_prompt_core.reshape(-1, D))
    m["xs"] = np.ascontiguousarray(x_sample_core.reshape(-1, D))
    m["scv"] = np.ascontiguousarray(state_conv_core.reshape(-1, 2048))
    m["cwk"] = np.ascontiguousarray(ck_core.reshape(-1, 512))
    m["cwv"] = np.ascontiguousarray(cv_core.reshape(-1, 512))
    return m


def shared_weights(cfg, norm_mix_pre, norm_mix_post, norm_ffn_pre, norm_ffn_post, w_in_even, w_out_even, sgu_w, sgu_b,
                   conv_w, w_qkv_odd, w_o_odd, attn_sinks, rel_bias, ffn_w_gate, ffn_w_up, ffn_w_down):
    f = lambda a: np.ascontiguousarray(np.asarray(a, dtype=np.float32))
    dp = cfg.depth
    g = np.zeros((16, D), np.float32)
    for t, a in enumerate((norm_mix_pre, norm_mix_post, norm_ffn_pre, norm_ffn_post)):
        g[t * 4:t * 4 + dp] = np.asarray(a)[:dp]
    W = {
        "gall": g,
        "w_in": f(w_in_even).reshape(-1, 10240),
        "w_out": f(w_out_even).reshape(-1, D),
        "sguw": f(sgu_w).reshape(-1, 128),
        "sgub": f(sgu_b).reshape(1, -1),
        "convw": f(conv_w).reshape(-1, 2048),
        "w_qkv": f(w_qkv_odd).reshape(-1, 5120),
        "w_o": f(w_o_odd).reshape(-1, D),
        "sinks": f(attn_sinks).reshape(1, -1),
        "relb": f(rel_bias),
        "wg": f(ffn_w_gate).reshape(-1, cfg.dff),
        "wu": f(ffn_w_up).reshape(-1, cfg.dff),
        "wd": f(ffn_w_down).reshape(-1, D),
        "cst": make_consts(),
    }
    return W


def kernel(x_prompt, x_sample, state_conv, cache_win_k, cache_win_v,
           norm_mix_pre, norm_mix_post, norm_ffn_pre, norm_ffn_post,
           w_in_even, w_out_even, sgu_w, sgu_b, conv_w,
           w_qkv_odd, w_o_odd, attn_sinks, rel_bias,
           ffn_w_gate, ffn_w_up, ffn_w_down):
    cfg = Cfg()
    x_prompt = np.asarray(x_prompt, np.float32)
    x_sample = np.asarray(x_sample, np.float32)
    state_conv = np.asarray(state_conv, np.float32)
    cache_win_k = np.asarray(cache_win_k, np.float32)
    cache_win_v = np.asarray(cache_win_v, np.float32)
    W = shared_weights(cfg, norm_mix_pre, norm_mix_post, norm_ffn_pre, norm_ffn_post, w_in_even, w_out_even, sgu_w, sgu_b,
                       conv_w, w_qkv_odd, w_o_odd, attn_sinks, rel_bias, ffn_w_gate, ffn_w_up, ffn_w_down)
    nc = build(cfg)
    in_maps = []
    for c in range(8):
        s, hh = c // 2, c % 2
        b0 = 0 if hh == 0 else 6
        xpc = x_prompt[s, b0 * 128:(b0 + 10) * 128]
        sl = slice(4 * c, 4 * c + 4)
        in_maps.append(host_inputs(cfg, xpc, x_sample[sl], state_conv[:, sl], cache_win_k[:, sl].reshape(2, 4, 128, 512),
                                   cache_win_v[:, sl].reshape(2, 4, 128, 512), W))
    res = run_bass_kernel_spmd(nc, in_maps, core_ids=list(range(8))).results
    B, S = 4, 2048
    y_p = np.zeros((B, S, D), np.float32)
    y_s = np.zeros((32, 4, D), np.float32)
    conv_p = np.zeros((2, B, 2, 2048), np.float32)
    conv_s = np.zeros((2, 32, 2, 2048), np.float32)
    kp = np.zeros((2, B, 128, 8, 64), np.float32)
    vp = np.zeros((2, B, 128, 8, 64), np.float32)
    ks = np.zeros((2, 32, 128, 8, 64), np.float32)
    vs = np.zeros((2, 32, 128, 8, 64), np.float32)
    cv = np.zeros((2, 32, 4, 8, 256), np.float32)
    for c in range(8):
        r = res[c]
        s, hh = c // 2, c % 2
        yp = r["o_yp"].reshape(10 * 128, D)
        if hh == 0:
            y_p[s, 0:1280] = yp
        else:
            y_p[s, 1280:2048] = yp[4 * 128:]
            conv_p[:, s] = r["o_cp"].reshape(2, 2, 2048)
            kp[:, s] = r["o_kp"].reshape(2, 128, 8, 64)
            vp[:, s] = r["o_vp"].reshape(2, 128, 8, 64)
        sl = slice(4 * c, 4 * c + 4)
        y_s[sl] = r["o_ys"].reshape(4, 4, D)
        conv_s[:, sl] = r["o_cs"].reshape(2, 4, 2, 2048)
        ks[:, sl] = r["o_ks"].reshape(2, 4, 128, 8, 64)
        vs[:, sl] = r["o_vs"].reshape(2, 4, 128, 8, 64)
        cv[:, sl] = r["o_cv"].reshape(2, 4, 4, 8, 256)
    return (y_p, y_s, conv_p, conv_s, kp, vp, ks, vs, cv)
```
